# Optimizing a Trainium2 kernel written in Bass

```python
import math
import jax, jax.numpy as jnp
from jax import lax
import numpy as np

D_MODEL = 1024
BATCH = 16
SEQ = 2048
DEPTH = 1
DEC_BATCH = 8
DEC_SEQ = 8192
PAST_LEN = 128

MIX_WIDTH = D_MODEL
HY_WIDTH = MIX_WIDTH // 2
ATT_WIDTH = MIX_WIDTH - HY_WIDTH
HEAD_DIM = 64
N_HEADS = ATT_WIDTH // HEAD_DIM
N_KV_HEADS = 2
GROUP = N_HEADS // N_KV_HEADS
HY_COLS = 3 * HY_WIDTH
Q_COLS = N_HEADS * HEAD_DIM
KV_COLS = N_KV_HEADS * HEAD_DIM
IN_COLS = HY_COLS + Q_COLS + 2 * KV_COLS
SHORT_CONV = 3
FILTER_EMB = 33
FILTER_BANDS = (FILTER_EMB - 1) // 2
FILTER_HIDDEN = 64
N_DIRS = 2
DECAY_TARGET = 1e-2
FAST_DECAY_PCT = 0.3
SLOW_DECAY_PCT = 1.5
GRID_W = 64
ROPE_THETA = 10000.0
AXIS_DIM = HEAD_DIM // 2
Q_BLOCK = 128
D_FF = ((8 * D_MODEL + 3 * 256 - 1) // (3 * 256)) * 256
EPS = 1e-6

kernel_name = 'hymba_hyena_gqa_axial_encoder'


def _rmsnorm(x, g):
    x32 = x.astype(jnp.float32)
    y = x32 * lax.rsqrt(jnp.mean(x32 * x32, axis=-1, keepdims=True) + EPS)
    return (y * g.astype(jnp.float32)).astype(x.dtype)


def _hyena_filter(L, w1, b1, w2, b2, w3, freq, decay):
    f32 = jnp.float32
    t = jnp.linspace(0.0, 1.0, L, dtype=f32)[:, None]
    w = (2.0 * math.pi / L) * jnp.arange(L, dtype=f32)
    bands = jnp.linspace(1e-4, FILTER_BANDS - 1, FILTER_BANDS, dtype=f32)
    ang = w[:, None] * bands[None, :]
    z = jnp.concatenate([t, jnp.cos(ang), -jnp.sin(ang)], axis=-1)
    fr = freq.astype(f32)
    h = jnp.sin(fr * (z @ w1.astype(f32) + b1.astype(f32)))
    h = jnp.sin(fr * (h @ w2.astype(f32) + b2.astype(f32)))
    h = (h @ w3.astype(f32)).reshape(L, N_DIRS, HY_WIDTH)
    h = h * jnp.exp(-t[:, :, None] * jnp.abs(decay.astype(f32))[None])
    h_fwd, h_bwd = h[:, 0], h[:, 1]
    k = jnp.concatenate([h_fwd, jnp.zeros((1, HY_WIDTH), f32), h_bwd[:0:-1]], axis=0)
    return k / jnp.sum(jnp.abs(k), axis=0, keepdims=True)


def _hyena(u, conv_w, conv_b, filt, d_bias):
    L = u.shape[1]
    pad = SHORT_CONV // 2
    up = jnp.pad(u, ((0, 0), (pad, pad), (0, 0)))
    uc = conv_b + sum(up[:, j:j + L] * conv_w[j] for j in range(SHORT_CONV))
    x0, x1, v = jnp.split(uc, 3, axis=-1)
    z = (x1 * v).astype(jnp.float32)
    n = 2 * L
    zf = jnp.fft.rfft(z, n=n, axis=1)
    kf = jnp.fft.rfft(filt, n=n, axis=0)
    y = jnp.fft.irfft(zf * kf[None], n=n, axis=1)[:, :L]
    y = y + z * d_bias.astype(jnp.float32)
    return (x0.astype(jnp.float32) * y).astype(u.dtype)


def _axial_rope(L):
    f32 = jnp.float32
    rows = L // GRID_W
    row = jnp.repeat(jnp.arange(rows, dtype=f32), GRID_W)
    col = jnp.tile(jnp.arange(GRID_W, dtype=f32), rows)
    inv = ROPE_THETA ** (-jnp.arange(0, AXIS_DIM, 2, dtype=f32) / AXIS_DIM)
    ang = jnp.concatenate([row[:, None] * inv, col[:, None] * inv], axis=-1)
    return jnp.cos(ang), jnp.sin(ang)


def _apply_rope(x, cos, sin):
    xp = x.reshape(x.shape[:-1] + (HEAD_DIM // 2, 2))
    a, b = xp[..., 0], xp[..., 1]
    c = cos[None, :, None, :]
    s = sin[None, :, None, :]
    return jnp.stack([a * c - b * s, a * s + b * c], axis=-1).reshape(x.shape)


def _attention(q, k, v):
    B, L = q.shape[:2]
    nb = L // Q_BLOCK
    qb = q.reshape(B, nb, Q_BLOCK, N_KV_HEADS, GROUP, HEAD_DIM).transpose(1, 0, 2, 3, 4, 5)
    scale = HEAD_DIM ** -0.5

    def block(qi):
        s = jnp.einsum('bqkgd,bskd->bkgqs', qi, k, preferred_element_type=jnp.float32) * scale
        p = jax.nn.softmax(s, axis=-1).astype(v.dtype)
        return jnp.einsum('bkgqs,bskd->bqkgd', p, v)

    o = lax.map(block, qb)
    return o.transpose(1, 0, 2, 3, 4, 5).reshape(B, L, N_HEADS * HEAD_DIM)


def _layer(x, norm_mix_g, w_in, hy_conv_w, hy_conv_b, hy_f_w1, hy_f_b1, hy_f_w2, hy_f_b2,
           hy_f_w3, hy_f_freq, hy_decay, hy_d, q_norm_g, k_norm_g, hy_out_g, att_out_g,
           w_out, norm_ffn_g, w_gate, w_up, w_down):
    B, L, _ = x.shape
    f32 = jnp.float32
    h = _rmsnorm(x, norm_mix_g)
    proj = h @ w_in
    o1 = HY_COLS
    o2 = o1 + Q_COLS
    o3 = o2 + KV_COLS
    u_hy = proj[..., :o1]
    q = proj[..., o1:o2].reshape(B, L, N_HEADS, HEAD_DIM)
    k = proj[..., o2:o3].reshape(B, L, N_KV_HEADS, HEAD_DIM)
    v = proj[..., o3:].reshape(B, L, N_KV_HEADS, HEAD_DIM)
    filt = _hyena_filter(L, hy_f_w1, hy_f_b1, hy_f_w2, hy_f_b2, hy_f_w3, hy_f_freq, hy_decay)
    y_hy = _hyena(u_hy, hy_conv_w, hy_conv_b, filt, hy_d)
    cos, sin = _axial_rope(L)
    q = _apply_rope(_rmsnorm(q, q_norm_g).astype(f32), cos, sin).astype(x.dtype)
    k = _apply_rope(_rmsnorm(k, k_norm_g).astype(f32), cos, sin).astype(x.dtype)
    y_att = _attention(q, k, v)
    mixed = jnp.concatenate([_rmsnorm(y_hy, hy_out_g), _rmsnorm(y_att, att_out_g)], axis=-1) @ w_out
    x = x + mixed
    h = _rmsnorm(x, norm_ffn_g)
    x = x + (jax.nn.silu(h @ w_gate) * (h @ w_up)) @ w_down
    return x


def _trunk(x, layer_params, final_norm_g):
    for l in range(DEPTH):
        x = _layer(x, *[p[l] for p in layer_params])
    return _rmsnorm(x, final_norm_g)


def setup_inputs(seed: int = 0) -> dict:
    key = jax.random.key(seed)
    ks = jax.random.split(key, 24)
    f32 = jnp.float32

    def nrm(k, shape, scale):
        return jax.random.normal(k, shape, f32) * scale

    def gain(k, shape):
        return 1.0 + 0.02 * jax.random.normal(k, shape, f32)

    lo = abs(math.log(DECAY_TARGET)) / SLOW_DECAY_PCT
    hi = abs(math.log(DECAY_TARGET)) / FAST_DECAY_PCT
    decay_base = jnp.linspace(lo, hi, HY_WIDTH, dtype=f32)[None, None, :]
    return {
        'x_prompt': nrm(ks[0], (BATCH, SEQ, D_MODEL), 1.0),
        'x_sample': nrm(ks[1], (DEC_BATCH, DEC_SEQ, D_MODEL), 1.0),
        'norm_mix_g': gain(ks[2], (DEPTH, D_MODEL)),
        'w_in': nrm(ks[3], (DEPTH, D_MODEL, IN_COLS), D_MODEL ** -0.5),
        'hy_conv_w': nrm(ks[4], (DEPTH, SHORT_CONV, HY_COLS), SHORT_CONV ** -0.5),
        'hy_conv_b': nrm(ks[5], (DEPTH, HY_COLS), 0.02),
        'hy_f_w1': nrm(ks[6], (DEPTH, FILTER_EMB, FILTER_HIDDEN), FILTER_EMB ** -0.5),
        'hy_f_b1': nrm(ks[7], (DEPTH, FILTER_HIDDEN), 0.1),
        'hy_f_w2': nrm(ks[8], (DEPTH, FILTER_HIDDEN, FILTER_HIDDEN), FILTER_HIDDEN ** -0.5),
        'hy_f_b2': nrm(ks[9], (DEPTH, FILTER_HIDDEN), 0.1),
        'hy_f_w3': nrm(ks[10], (DEPTH, FILTER_HIDDEN, N_DIRS * HY_WIDTH), FILTER_HIDDEN ** -0.5),
        'hy_f_freq': gain(ks[11], (DEPTH, FILTER_HIDDEN)),
        'hy_decay': decay_base * (1.0 + 0.05 * jax.random.normal(ks[12], (DEPTH, N_DIRS, HY_WIDTH), f32)),
        'hy_d': nrm(ks[13], (DEPTH, HY_WIDTH), 1.0),
        'q_norm_g': gain(ks[14], (DEPTH, HEAD_DIM)),
        'k_norm_g': gain(ks[15], (DEPTH, HEAD_DIM)),
        'hy_out_g': gain(ks[16], (DEPTH, HY_WIDTH)),
        'att_out_g': gain(ks[17], (DEPTH, ATT_WIDTH)),
        'w_out': nrm(ks[18], (DEPTH, MIX_WIDTH, D_MODEL), MIX_WIDTH ** -0.5),
        'norm_ffn_g': gain(ks[19], (DEPTH, D_MODEL)),
        'w_gate': nrm(ks[20], (DEPTH, D_MODEL, D_FF), D_MODEL ** -0.5),
        'w_up': nrm(ks[21], (DEPTH, D_MODEL, D_FF), D_MODEL ** -0.5),
        'w_down': nrm(ks[22], (DEPTH, D_FF, D_MODEL), D_FF ** -0.5),
        'final_norm_g': gain(ks[23], (D_MODEL,)),
    }


def reference(x_prompt, x_sample, norm_mix_g, w_in, hy_conv_w, hy_conv_b, hy_f_w1, hy_f_b1,
              hy_f_w2, hy_f_b2, hy_f_w3, hy_f_freq, hy_decay, hy_d, q_norm_g, k_norm_g,
              hy_out_g, att_out_g, w_out, norm_ffn_g, w_gate, w_up, w_down, final_norm_g):
    layer_params = (norm_mix_g, w_in, hy_conv_w, hy_conv_b, hy_f_w1, hy_f_b1, hy_f_w2, hy_f_b2,
                    hy_f_w3, hy_f_freq, hy_decay, hy_d, q_norm_g, k_norm_g, hy_out_g, att_out_g,
                    w_out, norm_ffn_g, w_gate, w_up, w_down)
    y_prompt = _trunk(x_prompt, layer_params, final_norm_g)
    y_sample = _trunk(x_sample, layer_params, final_norm_g)
    return (y_prompt, y_sample)
```

```python
from contextlib import ExitStack
import numpy as np
import ml_dtypes
import concourse.bass as bass
import concourse.mybir as mybir
from concourse.bass_utils import run_bass_kernel_spmd

F32 = mybir.dt.float32
BF16 = mybir.dt.bfloat16
I32 = mybir.dt.int32
ALU = mybir.AluOpType
AF = mybir.ActivationFunctionType
AX = mybir.AxisListType

ENGS = ["pe", "act", "dve", "pool", "sp"]
DMA_RING = {"sp": 12, "act": 8, "pool": 10}

NT = 12288
SEQS = [(0, 8192, 0), (8192, 2048, 1), (10240, 2048, 1)]
LENS = [(8192, 128, 64, 4), (2048, 32, 16, 16)]
EPS = 1e-6
TWO_PI = float(2 * np.pi)
DVE_PATH = False
FFT_STORE_Q = "act"
FUSE_DVE_WORK = True


class Op:
    __slots__ = ("eng", "fn", "reads", "writes", "dma", "deps", "ticket", "sem", "idx", "has_dep")


class Prog:
    def __init__(self, nc, es):
        self.nc = nc
        self.ops = []
        self.sem = {e: es.enter_context(nc.semaphore("s_" + e)) for e in ENGS}
        self.bar = es.enter_context(nc.semaphore("s_bar"))
        self.ring = {q: [es.enter_context(nc.semaphore("r_%s%d" % (q, i))) for i in range(n)]
                     for q, n in DMA_RING.items()}
        self.ticket = {e: 0 for e in ENGS}
        self.dma_n = {q: 0 for q in DMA_RING}
        self.known = {e: {} for e in ENGS}
        self.bar_n = 0

    def add(self, eng, fn, reads=(), writes=(), dma=False):
        op = Op()
        op.eng, op.fn, op.dma = eng, fn, dma
        op.reads = tuple(reads)
        op.writes = tuple(writes)
        op.idx = len(self.ops)
        op.has_dep = False
        op.ticket = None
        self.ops.append(op)
        return op

    def dma(self, q, out, in_, reads=(), writes=()):
        return self.add(q, lambda e: e.dma_start(out=out, in_=in_), reads, writes, dma=True)

    def flush(self):
        nc = self.nc
        ops = self.ops
        last_w, readers = {}, {}
        for op in ops:
            deps = set()
            for b in op.reads:
                w = last_w.get(b)
                if w is not None:
                    deps.add(w)
            for b in op.writes:
                w = last_w.get(b)
                if w is not None:
                    deps.add(w)
                for r in readers.get(b, ()):
                    deps.add(r)
            for b in op.reads:
                readers.setdefault(b, []).append(op)
            for b in op.writes:
                last_w[b] = op
                readers[b] = []
            deps.discard(op)
            if op.eng == "pe" and not op.dma:
                deps = {d for d in deps if not (d.eng == "pe" and not d.dma)}
            op.deps = deps
            for d in deps:
                d.has_dep = True
        per_eng = {e: [] for e in ENGS}
        for op in ops:
            per_eng[op.eng].append(op)
        for e in ENGS:
            lst = per_eng[e]
            comp = [o for o in lst if not o.dma]
            if comp:
                comp[-1].has_dep = True
            for o in lst:
                if o.dma:
                    n = self.dma_n[e]
                    R = len(self.ring[e])
                    o.sem = (self.ring[e][n % R], 16 * (n // R + 1))
                    self.dma_n[e] = n + 1
                elif o.has_dep:
                    self.ticket[e] += 1
                    o.ticket = self.ticket[e]
        self.bar_n += len(ENGS)
        bar_target = self.bar_n
        final_ticket = dict(self.ticket)
        prog = self

        def wait(engname, eng, sem, val):
            k = prog.known[engname]
            key = id(sem)
            if k.get(key, 0) >= val:
                return
            eng.wait_ge(sem, val)
            k[key] = val

        def emit(engname):
            def body(eng):
                for o in per_eng[engname]:
                    need = {}
                    for d in o.deps:
                        sm, val = (d.sem[0], d.sem[1]) if d.dma else (prog.sem[d.eng], d.ticket)
                        cur = need.get(id(sm))
                        if cur is None or cur[1] < val:
                            need[id(sm)] = (sm, val)
                    for sm, val in need.values():
                        wait(engname, eng, sm, val)
                    if o.dma:
                        sem, val = o.sem
                        if val > 16:
                            wait(engname, eng, sem, val - 16)
                        o.fn(eng).then_inc(sem, 16)
                    else:
                        ins = o.fn(eng)
                        if o.ticket is not None:
                            ins.then_inc(prog.sem[engname], 1)
                if engname in prog.ring:
                    n = prog.dma_n[engname]
                    R = len(prog.ring[engname])
                    for j in range(max(0, n - R), n):
                        wait(engname, eng, prog.ring[engname][j % R], 16 * (j // R + 1))
                if final_ticket[engname] > 0:
                    wait(engname, eng, prog.sem[engname], final_ticket[engname])
                eng.sem_inc(prog.bar, 1)
                eng.wait_ge(prog.bar, bar_target)
            return body

        with nc.Block() as block:
            block.tensor(emit("pe"))
            block.scalar(emit("act"))
            block.vector(emit("dve"))
            block.gpsimd(emit("pool"))
            block.sync(emit("sp"))
        self.ops = []


def ACT(P, r, w, out, in_, func, **kw):
    P.add("act", lambda e: e.activation(out=out, in_=in_, func=func, **kw), r, w)


def TT(P, r, w, out, a, b, op, eng="dve"):
    P.add(eng, lambda e: e.tensor_tensor(out=out, in0=a, in1=b, op=op), r, w)


def TS(P, r, w, out, a, s1, s2, op0, op1=None, eng="dve"):
    if op1 is None:
        P.add(eng, lambda e: e.tensor_scalar(out=out, in0=a, scalar1=s1, scalar2=None, op0=op0), r, w)
    else:
        P.add(eng, lambda e: e.tensor_scalar(out=out, in0=a, scalar1=s1, scalar2=s2, op0=op0, op1=op1), r, w)


def STT(P, r, w, out, a, s, b, op0, op1, eng="dve"):
    P.add(eng, lambda e: e.scalar_tensor_tensor(out=out, in0=a, scalar=s, in1=b, op0=op0, op1=op1), r, w)


def CP(P, r, w, out, in_, eng="dve"):
    P.add(eng, lambda e: e.tensor_copy(out=out, in_=in_), r, w)


def RCP(P, r, w, out, in_):
    P.add("dve", lambda e: e.reciprocal(out=out, in_=in_), r, w)


def MM(P, r, w, out, lhsT, rhs, start=True, stop=True):
    P.add("pe", lambda e: e.matmul(out=out, lhsT=lhsT, rhs=rhs, start=start, stop=stop), r, w)


def TR(P, r, w, out, in_, ident):
    P.add("pe", lambda e: e.transpose(out=out, in_=in_, identity=ident), r, w)


def MS(P, w, ap, val, eng="pool"):
    P.add(eng, lambda e: e.memset(ap, val), (), w)


class Ring:
    def __init__(self, alloc, name, shape, dt, n):
        self.t = [alloc("%s%d" % (name, i), shape, dt) for i in range(n)]
        self.n = n
        self.name = name

    def get(self, i):
        return self.t[i % self.n], "%s%d" % (self.name, i % self.n)


_PHASE_N = [0]


def allocators(nc, ph):
    _PHASE_N[0] += 1
    pre = "p%d_" % _PHASE_N[0]
    sb = lambda name, shape, dt: ph.enter_context(nc.sbuf_tensor(pre + name, list(shape), dt))
    ps = lambda name, shape, dt: ph.enter_context(nc.psum_tensor(pre + name, list(shape), dt))
    return sb, ps


def seq_of(tok0):
    for si, (off, L, li) in enumerate(SEQS):
        if off <= tok0 < off + L:
            return si, off, L, li
    raise ValueError


class WeightKeys:
    def __init__(self, name, cbw, ncb):
        self.name, self.cbw, self.ncb = name, cbw, ncb

    def __call__(self, c0, c1):
        out = []
        for cb in range(self.ncb):
            if cb * self.cbw < c1 and (cb + 1) * self.cbw > c0:
                out += ["%s_c%d" % (self.name, cb), "%s_c%da" % (self.name, cb)]
        return out


def load_scaled_weight(P, sb, name, dram, nk, cols, gt, gk, wst, cbw, q="sp"):
    W = sb(name, [128, nk, cols], BF16)
    ncb = (cols + cbw - 1) // cbw
    n = 0
    for cb in range(ncb):
        c0, c1 = cb * cbw, min(cols, (cb + 1) * cbw)
        key = "%s_c%d" % (name, cb)
        for k in range(nk):
            st, stk = wst.get(n)
            n += 1
            P.dma(q, st[:, 0:c1 - c0], dram[k * 128:(k + 1) * 128, c0:c1], writes=[stk])
            on_act = (n % 2 == 0)
            if gt is None:
                if on_act:
                    ACT(P, [stk], [key + ("a" if on_act else "")], W[:, k, c0:c1], st[:, 0:c1 - c0], AF.Copy)
                else:
                    CP(P, [stk], [key], W[:, k, c0:c1], st[:, 0:c1 - c0])
            else:
                if on_act:
                    ACT(P, [stk, gk], [key + "a"], W[:, k, c0:c1], st[:, 0:c1 - c0], AF.Copy, scale=gt[:, k:k + 1])
                else:
                    TS(P, [stk, gk], [key], W[:, k, c0:c1], st[:, 0:c1 - c0], gt[:, k:k + 1], None, ALU.mult)
    return W, WeightKeys(name, cbw, ncb)


def norm_part1(P, xt, xk, ti, rings, eps):
    ssq, xnr, pTr, junk = rings
    sq, sqk = ssq.get(ti)
    ACT(P, [xk], [sqk], junk[:], xt[:], AF.Square, accum_out=sq[:])
    ACT(P, [sqk, "eps"], [sqk], sq[:], sq[:], AF.Ln, bias=eps[:], scale=1.0 / 1024)
    ACT(P, [sqk], [sqk], sq[:], sq[:], AF.Exp, scale=-0.5)
    xn, xnk = xnr.get(ti)
    TS(P, [xk, sqk], [xnk], xn[:], xt[:], sq[:], None, ALU.mult)


def norm_part2(P, ti, j, rings, ident, hTt, hTk):
    ssq, xnr, pTr, junk = rings
    xn, xnk = xnr.get(ti)
    pT, pTk = pTr.get(ti)
    for k in range(8):
        TR(P, [xnk, "ident"], [pTk], pT[:, k, :], xn[:, k * 128:(k + 1) * 128], ident[:])
    CP(P, [pTk], ["%s_%d" % (hTk, j)], hTt[:, :, j * 128:(j + 1) * 128], pT[:])


class Pending:
    def __init__(self):
        self.items = []
        self.n = 0

    def add(self, delay, fn):
        self.items.append((self.n + delay, fn))

    def tick(self):
        self.n += 1
        due = [p for p in self.items if p[0] <= self.n]
        self.items = [p for p in self.items if p[0] > self.n]
        for p in due:
            p[1]()

    def drain(self):
        while self.items:
            self.tick()


def phase_A(P, nc, D):
    with ExitStack() as ph:
        sb, ps = allocators(nc, ph)
        wst = Ring(sb, "wst", [128, 608], F32, 4)
        gmix = sb("gmix", [128, 8], F32)
        P.dma("sp", gmix[:], D["g_mix"], writes=["gmix"])
        ident = sb("ident", [128, 128], BF16)
        pmat = sb("pmat", [128, 128], F32)
        bones = sb("bones", [128, 128], BF16)
        gq = sb("gq", [128, 2], F32)
        eps = sb("eps", [128, 1], F32)
        P.dma("sp", ident[:], D["ident"], writes=["ident"])
        P.dma("sp", pmat[:], D["pmat"], writes=["pmat"])
        P.dma("sp", bones[:], D["bones"], writes=["bones"])
        P.dma("sp", gq[:], D["gqk"], writes=["gq"])
        MS(P, ["eps"], eps[:], EPS)
        xr = Ring(sb, "xt", [128, 1024], F32, 3)
        junk = sb("junk", [128, 1024], BF16)
        ssq = Ring(sb, "ssq", [128, 1], F32, 4)
        xnr = Ring(sb, "xn", [128, 1024], BF16, 2)
        pTr = Ring(ps, "pT", [128, 8, 128], BF16, 1)
        hT = Ring(sb, "hT", [128, 8, 512], BF16, 2)
        cs = Ring(sb, "cs", [128, 2, 512], F32, 2)
        pin = Ring(ps, "pin", [128, 512], F32, 4)
        pss = ps("pss", [128, 512], F32)
        ppq = ps("ppq", [128, 512], F32)
        pv = ps("pv", [128, 512], F32)
        hst = Ring(sb, "hst", [128, 512], BF16, 4)
        qs = Ring(sb, "qs", [128, 512], F32, 3)
        qsq = Ring(sb, "qsq", [128, 512], BF16, 3)
        rq = Ring(sb, "rq", [128, 512], F32, 3)
        qg = Ring(sb, "qg", [128, 512], F32, 3)
        t1 = Ring(sb, "t1", [128, 512], F32, 3)
        t2 = Ring(sb, "t2", [128, 512], F32, 3)
        rot = Ring(sb, "rot", [128, 512], BF16, 4)
        vst = Ring(sb, "vst", [128, 2, 65], BF16, 2)
        for i in range(2):
            MS(P, ["vst%d" % i], vst.t[i][:], 1.0)
        rings = (ssq, xnr, pTr, junk)
        nst = NT // 512
        pend = Pending()
        qcnt = [0]

        def prep1(st, j):
            tok0 = st * 512
            ti = st * 4 + j
            xt, xk = xr.get(ti)
            P.dma("sp", xt[:], D["xs"][tok0 + j * 128:tok0 + (j + 1) * 128, :], writes=[xk])
            norm_part1(P, xt, xk, ti, rings, eps)

        def prep2(st, j):
            hTt, hTk = hT.get(st)
            norm_part2(P, st * 4 + j, j, rings, ident, hTt, hTk)

        def cs_load(st):
            tok0 = st * 512
            si, off, L, li = seq_of(tok0)
            tl = tok0 - off
            cst, csk = cs.get(st)
            P.dma("sp", cst[:, 0, :], D["ropec"][:, tl:tl + 512], writes=[csk + "c"])
            P.dma("sp", cst[:, 1, :], D["ropes"][:, tl:tl + 512], writes=[csk + "s"])

        def qk_post1(qn, gcol):
            a, ak = qs.get(qn)
            b, bk = qsq.get(qn)
            MM(P, [bk, "bones"], ["pss"], pss[:], bones[:], b[:])
            r_, rk = rq.get(qn)
            ACT(P, ["pss", "eps"], [rk], r_[:], pss[:], AF.Ln, bias=eps[:], scale=1.0)
            ACT(P, [rk], [rk], r_[:], r_[:], AF.Exp, scale=-0.5)
            g_, gk_ = qg.get(qn)
            STT(P, [ak, rk, "gq"], [gk_], g_[:], a[:], gcol, r_[:], ALU.mult, ALU.mult)

        def qk_post2(qn, st, qi):
            tok0 = st * 512
            cst, csk = cs.get(st)
            g_, gk_ = qg.get(qn)
            MM(P, [gk_, "pmat"], ["ppq"], ppq[:], pmat[:], g_[:])
            u1, u1k = t1.get(qn)
            u2, u2k = t2.get(qn)
            TT(P, [gk_, csk + "c"], [u1k], u1[:], g_[:], cst[:, 0, :], ALU.mult, eng="pool")
            TT(P, ["ppq", csk + "s"], [u2k], u2[:], ppq[:], cst[:, 1, :], ALU.mult)
            ro, rok = rot.get(qn)
            TT(P, [u1k, u2k], [rok], ro[:], u1[:], u2[:], ALU.add, eng="pool")
            if qi < 4:
                P.dma("pool", D["QT"][qi * 128:(qi + 1) * 128, tok0:tok0 + 512], ro[:], reads=[rok])
            else:
                P.dma("pool", D["KT"][(qi - 4) * 128:(qi - 3) * 128, tok0:tok0 + 512], ro[:], reads=[rok])

        cs_load(0)
        for j in range(4):
            prep1(0, j)
            prep2(0, j)
        Wsb, WsbK = load_scaled_weight(P, sb, "Wsb", D["w_in"], 8, 2432, gmix, "gmix", wst, 608)
        for st in range(nst):
            tok0 = st * 512
            hTt, hTk = hT.get(st)
            hkeys = ["%s_%d" % (hTk, j) for j in range(4)]
            for c in range(18):
                pt, pk = pin.get(st * 18 + c)
                for k in range(8):
                    MM(P, hkeys + WsbK(c * 128, (c + 1) * 128), [pk], pt[:], Wsb[:, k, c * 128:(c + 1) * 128], hTt[:, k, :],
                       start=(k == 0), stop=(k == 7))
                if c < 12:
                    ht, hk = hst.get(st * 12 + c)
                    if c % 2 == 0:
                        ACT(P, [pk], [hk], ht[:], pt[:], AF.Copy)
                    else:
                        CP(P, [pk], [hk], ht[:], pt[:])
                    P.dma("pool", D["U"][c * 128:(c + 1) * 128, tok0:tok0 + 512], ht[:], reads=[hk])
                else:
                    qi = c - 12
                    gcol = gq[:, 0:1] if qi < 4 else gq[:, 1:2]
                    qn = qcnt[0]
                    qcnt[0] += 1
                    a, ak = qs.get(qn)
                    b, bk = qsq.get(qn)
                    ACT(P, [pk], [ak], a[:], pt[:], AF.Copy)
                    ACT(P, [pk], [bk], b[:], pt[:], AF.Square)
                    pend.add(1, (lambda qn=qn, gcol=gcol: qk_post1(qn, gcol)))
                    pend.add(2, (lambda qn=qn, st=st, qi=qi: qk_post2(qn, st, qi)))
                pend.tick()
                if st + 1 < nst:
                    if c == 0:
                        cs_load(st + 1)
                    if c in (1, 5, 9, 13):
                        prep1(st + 1, (c - 1) // 4)
                    if c in (4, 8, 12, 16):
                        prep2(st + 1, (c - 4) // 4)
            for j in range(4):
                ti = st * 4 + j
                for k in range(8):
                    MM(P, [hkeys[j]] + WsbK(2304, 2432), ["pv"], pv[:, 0:128], hTt[:, k, j * 128:(j + 1) * 128],
                       Wsb[:, k, 2304:2432], start=(k == 0), stop=(k == 7))
                vt, vk = vst.get(ti)
                CP(P, ["pv"], [vk], vt[:, :, 0:64], pv[:, 0:128].rearrange("p (g d) -> p g d", g=2))
                P.dma("pool", D["VA"][tok0 + j * 128:tok0 + (j + 1) * 128, :],
                      vt[:].rearrange("p g d -> p (g d)"), reads=[vk])
                pend.tick()
        pend.drain()
        P.flush()


def sin_reduced(P, sb_tiles, n, keys_in, arg, argk, out, outk):
    ki, y = sb_tiles
    kit, kik = ki.get(n)
    yt, yk = y.get(n)
    TS(P, [argk], [kik], kit[:], arg, 1.0 / TWO_PI, None, ALU.mult)
    STT(P, [kik, argk], [yk], yt[:], kit[:], -TWO_PI, arg, ALU.mult, ALU.add)
    TS(P, [yk], [yk], yt[:], yt[:], float(np.pi), -float(np.pi), ALU.min, ALU.max)
    ACT(P, [yk], [outk], out, yt[:], AF.Sin)


def phase_F1(P, nc, D, li):
    L, N1, nblk, G = LENS[li]
    ntc = L // 512
    with ExitStack() as ph:
        sb, ps = allocators(nc, ph)
        w1 = sb("w1", [33, 64], F32)
        w2 = sb("w2", [64, 64], F32)
        w3 = sb("w3", [64, 1024], F32)
        w3b = sb("w3b", [64, 1024], BF16)
        fr = sb("fr", [64, 1], F32)
        b1 = sb("b1", [64, 1], F32)
        b2 = sb("b2", [64, 1], F32)
        ndec = sb("ndec", [128, 8], F32)
        P.dma("sp", w1[:], D["fw1"], writes=["w1"])
        P.dma("sp", w2[:], D["fw2"], writes=["w2"])
        P.dma("sp", w3[:], D["fw3"], writes=["w3f"])
        CP(P, ["w3f"], ["w3"], w3b[:], w3[:])
        P.dma("sp", fr[:], D["ffr"], writes=["fr"])
        P.dma("sp", b1[:], D["fb1"], writes=["b1"])
        P.dma("sp", b2[:], D["fb2"], writes=["b2"])
        P.dma("sp", ndec[:], D["fdec"], writes=["ndec"])
        TT(P, ["b1", "fr"], ["b1"], b1[:], b1[:], fr[:], ALU.mult)
        TT(P, ["b2", "fr"], ["b2"], b2[:], b2[:], fr[:], ALU.mult)
        ACT(P, ["ndec"], ["ndec"], ndec[:], ndec[:], AF.Abs)
        TS(P, ["ndec"], ["ndec"], ndec[:], ndec[:], -1.0, None, ALU.mult)
        zt = Ring(sb, "zt", [33, 512], F32, 4)
        tr = Ring(sb, "tr", [128, 512], F32, 2)
        p1 = Ring(ps, "p1", [64, 512], F32, 2)
        p2 = Ring(ps, "p2", [64, 512], F32, 2)
        p3 = Ring(ps, "p3", [128, 512], F32, 3)
        a1 = Ring(sb, "a1", [64, 512], F32, 4)
        ki = Ring(sb, "ki", [64, 512], I32, 4)
        yy = Ring(sb, "yy", [64, 512], F32, 4)
        h1 = Ring(sb, "h1", [64, 512], F32, 4)
        h2all = sb("h2all", [64, L], BF16)
        win = Ring(sb, "win", [128, 512], F32, 3)
        hk = Ring(sb, "hk", [128, 512], F32, 3)
        junk = sb("junk", [128, 512], F32)
        asum = sb("asum", [128, 8, ntc], F32)
        for t0 in range(0, ntc, 4):
            tcs = list(range(t0, min(t0 + 4, ntc)))
            for tc in tcs:
                z, zk = zt.get(tc)
                P.dma("sp", z[:], D["zt%d" % li][:, tc * 512:(tc + 1) * 512], writes=[zk])
                pp, ppk = p1.get(tc)
                MM(P, [zk, "w1"], [ppk], pp[:], w1[:], z[:])
                a, ak = a1.get(tc)
                ACT(P, [ppk, "fr", "b1"], [ak], a[:], pp[:], AF.Identity, bias=b1[:], scale=fr[:])
            for tc in tcs:
                a, ak = a1.get(tc)
                h, hk1 = h1.get(tc)
                sin_reduced(P, (ki, yy), tc, None, a[:], ak, h[:], hk1)
            for tc in tcs:
                h, hk1 = h1.get(tc)
                pp, ppk = p2.get(tc)
                MM(P, [hk1, "w2"], [ppk], pp[:], w2[:], h[:])
                a, ak = a1.get(tc)
                ACT(P, [ppk, "fr", "b2"], [ak], a[:], pp[:], AF.Identity, bias=b2[:], scale=fr[:])
            for tc in tcs:
                a, ak = a1.get(tc)
                sin_reduced(P, (ki, yy), tc, None, a[:], ak, h2all[:, tc * 512:(tc + 1) * 512], "h2_%d" % tc)
        units2 = [(tc, oc) for tc in range(ntc) for oc in range(8)]

        def emit_win(m):
            tc, oc = units2[m]
            t_, tk = tr.get(tc)
            if oc == 0:
                P.dma("sp", t_[:], D["trow%d" % li][:, tc * 512:(tc + 1) * 512], writes=[tk])
            w_, wk = win.get(m)
            ACT(P, [tk, "ndec"], [wk], w_[:], t_[:], AF.Exp, scale=ndec[:, oc:oc + 1])

        emit_win(0)
        for m, (tc, oc) in enumerate(units2):
            pt, pk = p3.get(m)
            MM(P, ["h2_%d" % tc, "w3"], [pk], pt[:], w3b[:, oc * 128:(oc + 1) * 128], h2all[:, tc * 512:(tc + 1) * 512])
            if m + 1 < len(units2):
                emit_win(m + 1)
            w_, wk = win.get(m)
            k_, kk = hk.get(m)
            TT(P, [pk, wk], [kk], k_[:], pt[:], w_[:], ALU.mult)
            if oc >= 4 and tc == 0:
                MS(P, [kk], k_[:, 0:1], 0.0, eng="dve")
            ACT(P, [kk], ["asum_%d_%d" % (oc, tc)], junk[:], k_[:], AF.Abs, accum_out=asum[:, oc, tc:tc + 1])
            P.dma("pool", D["KRAW%d" % li][oc * 128:(oc + 1) * 128, tc * 512:(tc + 1) * 512], k_[:], reads=[kk])
        tot = sb("tot", [128, 8], F32)
        rn = sb("rn", [128, 4], F32)
        allk = ["asum_%d_%d" % (oc, tc) for oc in range(8) for tc in range(ntc)]
        P.add("dve", lambda e: e.tensor_reduce(out=tot[:], in_=asum[:], axis=AX.X, op=ALU.add), allk, ["tot"])
        TT(P, ["tot"], ["rn"], rn[:], tot[:, 0:4], tot[:, 4:8], ALU.add)
        RCP(P, ["rn"], ["rn"], rn[:], rn[:])
        P.dma("pool", D["RN%d" % li], rn[:], reads=["rn"])
        P.flush()


def f2_units(P, nc, D, li, sb):
    L, N1, nblk, G = LENS[li]
    ntc = L // 512
    pre = "f2_%d_" % li
    rn = sb(pre + "rn", [128, 4], F32)
    hd = sb(pre + "hd", [128, 4], F32)
    P.dma("sp", rn[:], D["RN%d" % li], writes=[pre + "rn"])
    P.dma("sp", hd[:], D["fd"], writes=[pre + "hd"])
    kr = Ring(sb, pre + "kr", [128, 512], F32, 3)
    kb = Ring(sb, pre + "kb", [128, 512], BF16, 3)
    units = []

    def unit(m, oc, tc):
        a, ak = kr.get(m)
        b, bk = kb.get(m)
        P.dma("sp", a[:], D["KRAW%d" % li][oc * 128:(oc + 1) * 128, tc * 512:(tc + 1) * 512], writes=[ak])
        TS(P, [ak, pre + "rn"], [ak], a[:], a[:], rn[:, oc % 4:oc % 4 + 1], None, ALU.mult)
        if oc < 4 and tc == 0:
            TT(P, [ak, pre + "hd"], [ak], a[:, 0:1], a[:, 0:1], hd[:, oc:oc + 1], ALU.add)
        CP(P, [ak], [bk], b[:], a[:])
        P.dma("pool", D["KF%d" % li][oc * 128:(oc + 1) * 128, tc * 512:(tc + 1) * 512], b[:], reads=[bk])

    m = 0
    for oc in range(8):
        for tc in range(ntc):
            units.append(lambda m=m, oc=oc, tc=tc: unit(m, oc, tc))
            m += 1
    return units


def phase_F2(P, nc, D, li):
    with ExitStack() as ph:
        sb, ps = allocators(nc, ph)
        for u in f2_units(P, nc, D, li, sb):
            u()
        P.flush()


class FFTConsts:
    def __init__(self, P, sb, D, li, inverse):
        L, N1, nblk, G = LENS[li]
        s = "_%d" % li
        self.f1cat = sb("f1cat", [nblk, 2 * N1], BF16)
        self.tw = sb("tw", [128, 2, G * N1], F32)
        self.c2 = sb("c2", [128, 128], BF16)
        self.s2 = sb("s2", [128, 128], BF16)
        self.s2n = sb("s2n", [128, 128], BF16)
        P.dma("sp", self.f1cat[:], D["f1cat" + s], writes=["fc"])
        P.dma("sp", self.tw[:], D["tw" + s], writes=["fc"])
        P.dma("sp", self.c2[:], D["c2"], writes=["fc"])
        P.dma("sp", self.s2[:], D["s2"], writes=["fc"])
        P.dma("sp", self.s2n[:], D["s2n"], writes=["fc"])
        if inverse:
            self.cs2 = sb("cs2", [128, 256], BF16)
            self.sc2 = sb("sc2", [128, 256], BF16)
            self.twT = sb("twT", [128, 2, 512], F32)
            self.c1i = sb("c1i", [128, 64], BF16)
            self.s1ni = sb("s1ni", [128, 64], BF16)
            P.dma("sp", self.cs2[:], D["cs2"], writes=["fc"])
            P.dma("sp", self.sc2[:], D["sc2"], writes=["fc"])
            P.dma("sp", self.twT[:], D["twT" + s], writes=["fc"])
            P.dma("sp", self.c1i[:], D["c1i" + s], writes=["fc"])
            P.dma("sp", self.s1ni[:], D["s1ni" + s], writes=["fc"])


def cmul(P, rk, tmp4, n, v3, Xr, Xi, Kr, Ki, outr, outrk, outi, outik):
    (a1, a1k), (b1, b1k), (a2, a2k), (b2, b2k) = [r.get(n) for r in tmp4]
    TT(P, rk, [a1k], v3(a1), Xr, Kr, ALU.mult)
    TT(P, rk, [b1k], v3(b1), Xi, Ki, ALU.mult)
    TT(P, rk, [a2k], v3(a2), Xr, Ki, ALU.mult)
    TT(P, rk, [b2k], v3(b2), Xi, Kr, ALU.mult)
    TT(P, [a1k, b1k], [outrk], outr, a1[:], b1[:], ALU.subtract, eng="pool")
    TT(P, [a2k, b2k], [outik], outi, a2[:], b2[:], ALU.add, eng="pool")


def tmp4(sb, name):
    return [Ring(sb, "%s%d" % (name, i), [128, 512], F32, 2) for i in range(4)]


def fft_P1(P, C, li, zb, zbk, psA, pk="psA"):
    L, N1, nblk, G = LENS[li]
    for ch in range(G):
        MM(P, [zbk, "fc"], [pk], psA[:, ch * 2 * N1:(ch + 1) * 2 * N1], zb[:, ch, :], C.f1cat[:])


def fft_D2(P, C, li, psA, t4, n, rp, rpk, ip, ipk, pk="psA"):
    L, N1, nblk, G = LENS[li]
    Av = psA[:].rearrange("p (g r f) -> p g r f", g=G, r=2)
    v3 = lambda t: t[:].rearrange("p (g f) -> p g f", g=G)
    Wr = C.tw[:, 0, :].rearrange("p (g f) -> p g f", g=G)
    Wi = C.tw[:, 1, :].rearrange("p (g f) -> p g f", g=G)
    cmul(P, [pk, "fc"], t4, n, v3, Av[:, :, 0, :], Av[:, :, 1, :], Wr, Wi, rp[:], rpk, ip[:], ipk)


def fft_P3(P, C, rp, rpk, ip, ipk, psX, xk="psX"):
    MM(P, [rpk, "fc"], [xk], psX[:, 0, :], C.c2[:], rp[:], start=True, stop=False)
    MM(P, [ipk, "fc"], [xk], psX[:, 0, :], C.s2[:], ip[:], start=False, stop=True)
    MM(P, [rpk, "fc"], [xk], psX[:, 1, :], C.s2n[:], rp[:], start=True, stop=False)
    MM(P, [ipk, "fc"], [xk], psX[:, 1, :], C.c2[:], ip[:], start=False, stop=True)


def phase_F3(P, nc, D, li):
    L, N1, nblk, G = LENS[li]
    GN = G * N1
    with ExitStack() as ph:
        sb, ps = allocators(nc, ph)
        C = FFTConsts(P, sb, D, li, inverse=False)
        zbr = Ring(sb, "zb", [nblk, G, 128], BF16, 4)
        psA = Ring(ps, "psA", [128, 2 * GN], F32, 2)
        psXr = Ring(ps, "psX", [128, 2, GN], F32, 2)
        t4 = tmp4(sb, "ta")
        arp = Ring(sb, "arp", [128, GN], BF16, 2)
        aip = Ring(sb, "aip", [128, GN], BF16, 2)
        xf = Ring(sb, "xf", [128, 2, GN], F32, 2)
        ko = Ring(sb, "ko", [128, G, 2, N1], F32, 2)
        nu = 2 * (512 // G)

        def load(u):
            gi, d = divmod(u, 2)
            c0 = d * 512 + gi * G
            z, zk = zbr.get(u)
            P.dma("sp", z[:], D["KF%d" % li][c0:c0 + G, :].rearrange("c (b i) -> b c i", i=128), writes=[zk])

        def P1(u):
            z, zk = zbr.get(u)
            a_, ak_ = psA.get(u)
            fft_P1(P, C, li, z, zk, a_, pk=ak_)

        def D2(u):
            rp, rpk = arp.get(u)
            ip, ipk = aip.get(u)
            a_, ak_ = psA.get(u)
            fft_D2(P, C, li, a_, t4, u, rp, rpk, ip, ipk, pk=ak_)

        def P3(u):
            rp, rpk = arp.get(u)
            ip, ipk = aip.get(u)
            x_, xk_ = psXr.get(u)
            fft_P3(P, C, rp, rpk, ip, ipk, x_, xk=xk_)

        def D4(u):
            gi, d = divmod(u, 2)
            psX, pxk = psXr.get(u)
            x_, xk = xf.get(gi)
            if d == 0:
                ACT(P, [pxk], [xk], x_[:], psX[:], AF.Copy)
            else:
                k_, kk = ko.get(gi)
                g3 = lambda a: a.rearrange("p (g f) -> p g f", g=G)
                TT(P, [pxk, xk], [kk + "r"], k_[:, :, 0, :], g3(psX[:, 0, :]), g3(x_[:, 0, :]), ALU.add)
                STT(P, [pxk, xk], [kk + "i"], k_[:, :, 1, :], g3(psX[:, 1, :]), -1.0, g3(x_[:, 1, :]), ALU.mult, ALU.add)
                c0 = gi * G
                P.dma(FFT_STORE_Q, D["KSPEC%d" % li][:, c0:c0 + G, :, :], k_[:], reads=[kk + "r", kk + "i"])

        load(0)
        load(1)
        P1(0)
        for t in range(nu + 2):
            if t + 2 < nu:
                load(t + 2)
            if t < nu:
                D2(t)
            if 0 <= t - 1 < nu:
                P3(t - 1)
                D4(t - 1)
            if t + 1 < nu:
                P1(t + 1)
        P.flush()


def b1_units(P, nc, D, sb):
    cw = sb("b1cw", [128, 4, 12], F32)
    P.dma("sp", cw[:], D["cw"], writes=["b1cw"])
    ub = Ring(sb, "b1ub", [128, 514], BF16, 6)
    ta = Ring(sb, "b1ta", [128, 512], F32, 3)
    uc = Ring(sb, "b1uc", [128, 512], F32, 3)
    zo = Ring(sb, "b1zo", [128, 512], BF16, 3)
    xo = Ring(sb, "b1xo", [128, 512], BF16, 3)

    def unit(q, st, j):
        tok0 = st * 512
        si, off, L, li = seq_of(tok0)
        tl = tok0 - off
        res = []
        for part in range(3):
            n = q * 3 + part
            c = part * 4 + j
            u, uk = ub.get(n)
            lo = 0 if tl > 0 else 1
            hi = 514 if tl + 512 < L else 513
            if lo == 1:
                MS(P, [uk + "l"], u[:, 0:1], 0.0)
            if hi == 513:
                MS(P, [uk + "h"], u[:, 513:514], 0.0)
            P.dma("sp", u[:, lo:hi], D["U"][c * 128:(c + 1) * 128, tok0 - 1 + lo:tok0 - 1 + hi],
                  reads=[uk + "l", uk + "h"], writes=[uk])
            rk = [uk, uk + "l", uk + "h", "b1cw"]
            a, ak = ta.get(n)
            o, ok = uc.get(n)
            TS(P, rk, [ak], a[:], u[:, 0:512], cw[:, 0, c:c + 1], cw[:, 3, c:c + 1], ALU.mult, ALU.add)
            STT(P, rk + [ak], [ak], a[:], u[:, 1:513], cw[:, 1, c:c + 1], a[:], ALU.mult, ALU.add)
            if part == 0:
                xt, xk = xo.get(q)
                STT(P, rk + [ak], [xk], xt[:], u[:, 2:514], cw[:, 2, c:c + 1], a[:], ALU.mult, ALU.add)
                P.dma("pool", D["X0"][j * 128:(j + 1) * 128, tok0:tok0 + 512], xt[:], reads=[xk])
            else:
                STT(P, rk + [ak], [ok], o[:], u[:, 2:514], cw[:, 2, c:c + 1], a[:], ALU.mult, ALU.add)
                res.append((o, ok))
        zt, zk = zo.get(q)
        TT(P, [res[0][1], res[1][1]], [zk], zt[:], res[0][0][:], res[1][0][:], ALU.mult)
        P.dma("pool", D["Z"][j * 128:(j + 1) * 128, tok0:tok0 + 512], zt[:], reads=[zk])

    units = []
    q = 0
    for st in range(NT // 512):
        for j in range(4):
            units.append(lambda q=q, st=st, j=j: unit(q, st, j))
            q += 1
    return units


def phase_B1(P, nc, D):
    with ExitStack() as ph:
        sb, ps = allocators(nc, ph)
        for u in b1_units(P, nc, D, sb):
            u()
        P.flush()


def phase_B2(P, nc, D, si):
    off, L, li = SEQS[si]
    _, N1, nblk, G = LENS[li]
    GN = G * N1
    ng = 512 // G
    with ExitStack() as ph:
        sb, ps = allocators(nc, ph)
        C = FFTConsts(P, sb, D, li, inverse=True)
        zbr = Ring(sb, "zb", [nblk, G, 128], BF16, 4)
        ksp = Ring(sb, "ksp", [128, G, 2, N1], F32, 4)
        psA = ps("psA", [128, 2 * GN], F32)
        psX = ps("psX", [128, 2, GN], F32)
        psB = ps("psB", [128, 4, 2, 128], F32)
        psYr = Ring(ps, "psY", [64, 512], F32, 2)
        ta, tb, tc = tmp4(sb, "ta"), tmp4(sb, "tb"), tmp4(sb, "tc")
        arp = Ring(sb, "arp", [128, GN], BF16, 2)
        aip = Ring(sb, "aip", [128, GN], BF16, 2)
        yr = Ring(sb, "yr", [128, GN], BF16, 2)
        yi = Ring(sb, "yi", [128, GN], BF16, 2)
        brp = Ring(sb, "brp", [128, 512], BF16, 2)
        bip = Ring(sb, "bip", [128, 512], BF16, 2)
        yo = Ring(sb, "yo", [64, 4, 128], F32, 3)

        def load(g):
            c0 = g * G
            z, zk = zbr.get(g)
            P.dma("sp", z[:], D["Z"][c0:c0 + G, off:off + L].rearrange("c (b i) -> b c i", i=128), writes=[zk])
            k_, kk = ksp.get(g)
            P.dma("sp", k_[:], D["KSPEC%d" % li][:, c0:c0 + G, :, :], writes=[kk])

        def P1(g):
            z, zk = zbr.get(g)
            fft_P1(P, C, li, z, zk, psA)

        def D2(g):
            rp, rpk = arp.get(g)
            ip, ipk = aip.get(g)
            fft_D2(P, C, li, psA, ta, g, rp, rpk, ip, ipk)

        def P3(g):
            rp, rpk = arp.get(g)
            ip, ipk = aip.get(g)
            fft_P3(P, C, rp, rpk, ip, ipk, psX)

        def D4(g):
            k_, kk = ksp.get(g)
            v3 = lambda t: t[:].rearrange("p (g f) -> p g f", g=G)
            g3 = lambda a: a.rearrange("p (g f) -> p g f", g=G)
            y_r, yrk = yr.get(g)
            y_i, yik = yi.get(g)
            cmul(P, ["psX", kk], tb, g, v3, g3(psX[:, 0, :]), g3(psX[:, 1, :]), k_[:, :, 0, :], k_[:, :, 1, :],
                 y_r[:], yrk, y_i[:], yik)

        def P5(g):
            y_r, yrk = yr.get(g)
            y_i, yik = yi.get(g)
            for u in range(4):
                lr, l_i = y_r[:, u * 128:(u + 1) * 128], y_i[:, u * 128:(u + 1) * 128]
                ob = psB[:, u, :, :].rearrange("p r s -> p (r s)")
                MM(P, [yrk, "fc"], ["psB"], ob, lr, C.cs2[:], start=True, stop=False)
                MM(P, [yik, "fc"], ["psB"], ob, l_i, C.sc2[:], start=False, stop=True)

        def D6(g):
            v3 = lambda t: t[:].rearrange("p (g s) -> p g s", g=4)
            Tc = C.twT[:, 0, :].rearrange("p (g s) -> p g s", g=4)
            Ts_ = C.twT[:, 1, :].rearrange("p (g s) -> p g s", g=4)
            rp, rpk = brp.get(g)
            ip, ipk = bip.get(g)
            cmul(P, ["psB", "fc"], tc, g, v3, psB[:, :, 0, :], psB[:, :, 1, :], Tc, Ts_, rp[:], rpk, ip[:], ipk)

        def P7(g):
            rp, rpk = brp.get(g)
            ip, ipk = bip.get(g)
            psY, pyk = psYr.get(g)
            for u in range(4):
                MM(P, [rpk, "fc"], [pyk], psY[:, u * 128:(u + 1) * 128], C.c1i[:], rp[:, u * 128:(u + 1) * 128],
                   start=True, stop=False)
                MM(P, [ipk, "fc"], [pyk], psY[:, u * 128:(u + 1) * 128], C.s1ni[:], ip[:, u * 128:(u + 1) * 128],
                   start=False, stop=True)
            y_, yk = yo.get(g)
            ACT(P, [pyk], [yk], y_[:].rearrange("p g s -> p (g s)"), psY[:], AF.Copy)
            c0 = g * G
            if li == 0:
                P.dma(FFT_STORE_Q, D["YC"][c0:c0 + 4, off:off + L].rearrange("c (b i) -> b c i", i=128), y_[:], reads=[yk])
            else:
                ycv = D["YC"][c0:c0 + 16, off:off + L].rearrange("(s c) (b i) -> c b s i", c=4, i=128)
                for c4 in range(4):
                    P.dma(FFT_STORE_Q, ycv[c4], y_[c4 * 16:(c4 + 1) * 16, :, :], reads=[yk])

        load(0)
        load(1)
        P1(0)
        for t in range(ng + 4):
            if t + 2 < ng:
                load(t + 2)
            if t < ng:
                D2(t)
            if 0 <= t - 1 < ng:
                P3(t - 1)
                D4(t - 1)
            if 0 <= t - 2 < ng:
                P5(t - 2)
                D6(t - 2)
            if 0 <= t - 3 < ng:
                P7(t - 3)
            if t + 1 < ng:
                P1(t + 1)
        P.flush()


def phase_C(P, nc, D, sis, extra=None):
    with ExitStack() as ph:
        sb, ps = allocators(nc, ph)
        sel = sb("sel", [65, 64], F32)
        P.dma("sp", sel[:], D["sel"], writes=["sel"])
        qt = Ring(sb, "qt", [128, 512], BF16, 3)
        psS = Ring(ps, "psS", [128, 512], F32, 4)
        psO = [ps("psO%d" % i, [65, 512], F32) for i in range(2)]
        psD = Ring(ps, "psD", [64, 512], F32, 2)
        pt = Ring(sb, "pt", [128, 512], BF16, 6)
        osb = Ring(sb, "osb", [65, 512], F32, 4)
        rden = Ring(sb, "rden", [64, 512], F32, 2)
        on = Ring(sb, "on", [64, 512], F32, 3)
        kvs = {}
        kv_loads = {}
        for si in sis:
            off, L, li = SEQS[si]
            nkb = L // 128
            va2 = sb("va2_%d" % si, [128, nkb, 130], BF16)
            lst = []
            loads = []
            for g in range(2):
                k_ = sb("kT_%d_%d" % (si, g), [128, L], BF16)
                kk = "kT_%d_%d" % (si, g)
                loads.append((lambda k_=k_, kk=kk, g=g, off=off, L=L:
                              P.dma("sp", k_[:], D["KT"][g * 128:(g + 1) * 128, off:off + L], writes=[kk])))
                if g == 0:
                    loads.append((lambda va2=va2, si=si, off=off, L=L:
                                  P.dma("sp", va2[:], D["VA"][off:off + L, :].rearrange("(b p) d -> p b d", p=128),
                                        writes=["va2_%d" % si])))
                lst.append((k_, kk, va2[:, :, g * 65:(g + 1) * 65], "va2_%d" % si))
            kvs[si] = lst
            kv_loads[si] = loads
        for f in kv_loads[sis[0]][:2]:
            f()
        units = extra(sb) if extra is not None else []
        pending = []
        st = {"gi": 0, "n": 0, "ui": 0, "qb": 0}

        for si in sis:
            off, L, li = SEQS[si]
            nkb = L // 128
            nqc = L // 512
            kv = kvs[si]
            blocks = [(g, hp, qc) for g in range(2) for hp in range(2) for qc in range(nqc)]
            iters = [(bi, kb) for bi in range(len(blocks)) for kb in range(nkb)]
            use_units = units if si == sis[0] else []
            every = max(1, len(iters) // max(1, len(use_units))) if use_units else 0
            qtiles = {}
            g0 = st["gi"]
            qb0 = st["qb"]

            def emit_S(i):
                bi, kb = iters[i]
                g, hp, qc = blocks[bi]
                j = g * 2 + hp
                k_, kk, v_, vk = kv[g]
                if kb == 0:
                    q_, qk = qt.get(qb0 + bi)
                    P.dma("sp", q_[:], D["QT"][j * 128:(j + 1) * 128, off + qc * 512:off + (qc + 1) * 512], writes=[qk])
                    qtiles[bi] = (q_, qk)
                q_, qk = qtiles[bi]
                for hh in range(2):
                    r0 = hh * 64
                    s_, sk = psS.get(2 * (g0 + i) + hh)
                    MM(P, [kk, qk], [sk], s_[:], k_[r0:r0 + 64, kb * 128:(kb + 1) * 128], q_[r0:r0 + 64, :])

            def epilogue2(g, hp, qc, hh, ob, obk, n, off=off):
                h = 2 * (g * 2 + hp) + hh
                pd, pdk = psD.get(n)
                MM(P, [obk, "sel"], [pdk], pd[:], sel[:], ob[:])
                rd, rdk = rden.get(n)
                RCP(P, [pdk], [rdk], rd[:], pd[:])
                o2, o2k = on.get(n)
                TT(P, [obk, rdk], [o2k], o2[:], ob[0:64, :], rd[:], ALU.mult, eng="pool")
                P.dma("pool", D["YATT"][h * 64:(h + 1) * 64, off + qc * 512:off + (qc + 1) * 512], o2[:], reads=[o2k])

            emit_S(0)
            if si == sis[0]:
                kv_loads[si][2]()
            for i in range(len(iters)):
                if si == sis[0] and i == 24:
                    for sj in sis[1:]:
                        for f in kv_loads[sj]:
                            f()
                bi, kb = iters[i]
                g, hp, qc = blocks[bi]
                k_, kk, v_, vk = kv[g]
                gi = g0 + i
                if i + 1 < len(iters):
                    emit_S(i + 1)
                sks = [psS.get(2 * gi + hh)[1] for hh in range(2)]
                for hh in range(2):
                    s_, sk = psS.get(2 * gi + hh)
                    p_, pk = pt.get(2 * gi + hh)
                    ACT(P, sks if hh == 0 else [sk], [pk], p_[:], s_[:], AF.Exp, scale=0.125)
                for hh in range(2):
                    p_, pk = pt.get(2 * gi + hh)
                    MM(P, [vk, pk], ["psO%d" % hh], psO[hh][:], v_[:, kb, :], p_[:], start=(kb == 0), stop=(kb == nkb - 1))
                still = []
                for (at, fn) in pending:
                    if at <= gi:
                        fn()
                    else:
                        still.append((at, fn))
                pending = still
                if kb == nkb - 1:
                    for hh in range(2):
                        n = st["n"]
                        ob, obk = osb.get(n)
                        CP(P, ["psO%d" % hh], [obk], ob[:], psO[hh][:])
                        pending.append((gi + 2 + 3 * hh, (lambda g=g, hp=hp, qc=qc, hh=hh, ob=ob, obk=obk, n=n, f=epilogue2:
                                                 f(g, hp, qc, hh, ob, obk, n))))
                        st["n"] += 1
                if use_units and i % every == min(4, every - 1) and st["ui"] < len(use_units):
                    use_units[st["ui"]]()
                    st["ui"] += 1
            st["gi"] += len(iters)
            st["qb"] += len(blocks)
            while use_units and st["ui"] < len(use_units):
                use_units[st["ui"]]()
                st["ui"] += 1
        for (at, fn) in pending:
            fn()
        P.flush()


def phase_D(P, nc, D):
    with ExitStack() as ph:
        sb, ps = allocators(nc, ph)
        wst = Ring(sb, "wst", [128, 512], F32, 4)
        gout = sb("gout", [128, 8], F32)
        P.dma("sp", gout[:], D["g_out"], writes=["gout"])
        Wo, WoK = load_scaled_weight(P, sb, "Wo", D["w_out"], 8, 1024, gout, "gout", wst, 512)
        onesm = sb("onesm", [128, 128], BF16)
        eps = sb("eps", [128, 1], F32)
        P.dma("sp", onesm[:], D["ones512"], writes=["onesm"])
        MS(P, ["eps"], eps[:], EPS)
        yc = Ring(sb, "yc", [128, 4, 512], F32, 2)
        x0 = Ring(sb, "x0", [128, 4, 512], BF16, 2)
        ya = Ring(sb, "ya", [128, 4, 512], F32, 2)
        yh = Ring(sb, "yh", [128, 4, 512], F32, 2)
        sq = Ring(sb, "sq", [128, 4, 512], BF16, 2)
        pss = Ring(ps, "pss", [128, 512], F32, 2)
        rr = Ring(sb, "rr", [128, 512], F32, 2)
        mx = Ring(sb, "mx", [128, 8, 512], BF16, 2)
        po = Ring(ps, "po", [128, 1024], F32, 2)
        xr = Ring(sb, "xt", [128, 1024], F32, 3)
        x2 = Ring(sb, "x2", [128, 1024], F32, 3)
        nst = NT // 512

        def prepA(st):
            tok0 = st * 512
            c_, ck = yc.get(st)
            x_, xk = x0.get(st)
            a_, ak = ya.get(st)
            P.dma("sp", c_[:], D["YC"][:, tok0:tok0 + 512].rearrange("(c p) t -> p c t", p=128), writes=[ck])
            P.dma("sp", x_[:], D["X0"][:, tok0:tok0 + 512].rearrange("(c p) t -> p c t", p=128), writes=[xk])
            P.dma("sp", a_[:], D["YATT"][:, tok0:tok0 + 512].rearrange("(c p) t -> p c t", p=128), writes=[ak])
            h_, hk = yh.get(st)
            TT(P, [ck, xk], [hk], h_[:], c_[:], x_[:], ALU.mult)
            for half, (src, srck) in enumerate(((h_, hk), (a_, ak))):
                s_, sk = sq.get(2 * st + half)
                ACT(P, [srck], [sk], s_[:], src[:], AF.Square)

        def prepB(st):
            a_, ak = ya.get(st)
            h_, hk = yh.get(st)
            m_, mk = mx.get(st)
            for half, (src, srck) in enumerate(((h_, hk), (a_, ak))):
                s_, sk = sq.get(2 * st + half)
                p_, pk = pss.get(2 * st + half)
                for c in range(4):
                    MM(P, [sk, "onesm"], [pk], p_[:], onesm[:], s_[:, c, :], start=(c == 0), stop=(c == 3))
                r_, rk = rr.get(2 * st + half)
                ACT(P, [pk, "eps"], [rk], r_[:], p_[:], AF.Ln, bias=eps[:], scale=1.0)
                ACT(P, [rk], [rk], r_[:], r_[:], AF.Exp, scale=-0.5)
                for c in range(4):
                    TT(P, [srck, rk], ["%s_%d" % (mk, half * 4 + c)], m_[:, half * 4 + c, :], src[:, c, :], r_[:],
                       ALU.mult, eng=("dve" if c % 2 == 0 else "pool"))

        def mm_tile(st, j):
            tok0 = st * 512
            m_, mk = mx.get(st)
            mkeys = ["%s_%d" % (mk, c) for c in range(8)]
            ti = st * 4 + j
            xt, xtk = xr.get(ti)
            P.dma("sp", xt[:], D["xs"][tok0 + j * 128:tok0 + (j + 1) * 128, :], writes=[xtk])
            o_, ok = po.get(ti)
            for hf in range(2):
                for c in range(8):
                    MM(P, mkeys + WoK(hf * 512, (hf + 1) * 512), [ok], o_[:, hf * 512:(hf + 1) * 512], m_[:, c, j * 128:(j + 1) * 128],
                       Wo[:, c, hf * 512:(hf + 1) * 512], start=(c == 0), stop=(c == 7))
            y_, yk = x2.get(ti)
            TT(P, [ok, xtk], [yk], y_[:], o_[:], xt[:], ALU.add)
            P.dma("pool", D["X2"][tok0 + j * 128:tok0 + (j + 1) * 128, :], y_[:], reads=[yk])

        prepA(0)
        prepB(0)
        for st in range(nst):
            if st + 1 < nst:
                prepA(st + 1)
            mm_tile(st, 0)
            mm_tile(st, 1)
            if st + 1 < nst:
                prepB(st + 1)
            mm_tile(st, 2)
            mm_tile(st, 3)
        P.flush()


def phase_E1(P, nc, D):
    with ExitStack() as ph:
        sb, ps = allocators(nc, ph)
        wst = Ring(sb, "wst", [128, 704], F32, 4)
        gf = sb("gf", [128, 8], F32)
        P.dma("sp", gf[:], D["g_ffn"], writes=["gf"])
        ident = sb("ident", [128, 128], BF16)
        eps = sb("eps", [128, 1], F32)
        P.dma("sp", ident[:], D["ident"], writes=["ident"])
        MS(P, ["eps"], eps[:], EPS)
        xr = Ring(sb, "xt", [128, 1024], F32, 3)
        junk = sb("junk", [128, 1024], BF16)
        ssq = Ring(sb, "ssq", [128, 1], F32, 4)
        xnr = Ring(sb, "xn", [128, 1024], BF16, 2)
        pTr = Ring(ps, "pT", [128, 8, 128], BF16, 2)
        hT = Ring(sb, "hT", [128, 8, 512], BF16, 2)
        pg = Ring(ps, "pg", [128, 512], F32, 3)
        pu = Ring(ps, "pu", [128, 512], F32, 3)
        sg = Ring(sb, "sg", [128, 512], F32, 3)
        ao = Ring(sb, "ao", [128, 512], BF16, 4)
        rings = (ssq, xnr, pTr, junk)
        nst = NT // 512

        def prep1(st, j):
            tok0 = st * 512
            ti = st * 4 + j
            xt, xk = xr.get(ti)
            P.dma("sp", xt[:], D["X2"][tok0 + j * 128:tok0 + (j + 1) * 128, :], writes=[xk])
            norm_part1(P, xt, xk, ti, rings, eps)

        def prep2(st, j):
            hTt, hTk = hT.get(st)
            norm_part2(P, st * 4 + j, j, rings, ident, hTt, hTk)

        for j in range(4):
            prep1(0, j)
            prep2(0, j)
        Wg = sb("Wg", [128, 8, 2816], BF16)
        Wu = sb("Wu", [128, 8, 2816], BF16)
        WgK, WuK = WeightKeys("Wg", 704, 4), WeightKeys("Wu", 704, 4)
        wn = 0
        for cb in range(4):
            for (W_, nm, dr) in ((Wg, "Wg", D["w_gate"]), (Wu, "Wu", D["w_up"])):
                for k in range(8):
                    st_, stk = wst.get(wn)
                    wn += 1
                    P.dma("sp", st_[:], dr[k * 128:(k + 1) * 128, cb * 704:(cb + 1) * 704], writes=[stk])
                    if wn % 2 == 0:
                        ACT(P, [stk, "gf"], ["%s_c%da" % (nm, cb)], W_[:, k, cb * 704:(cb + 1) * 704], st_[:], AF.Copy,
                            scale=gf[:, k:k + 1])
                    else:
                        TS(P, [stk, "gf"], ["%s_c%d" % (nm, cb)], W_[:, k, cb * 704:(cb + 1) * 704], st_[:], gf[:, k:k + 1],
                           None, ALU.mult)
        m = 0
        for st in range(nst):
            tok0 = st * 512
            hTt, hTk = hT.get(st)
            hkeys = ["%s_%d" % (hTk, j) for j in range(4)]
            for fc in range(22):
                g_, gk = pg.get(m)
                u_, uk = pu.get(m)
                for k in range(8):
                    MM(P, hkeys + WgK(fc * 128, (fc + 1) * 128), [gk], g_[:], Wg[:, k, fc * 128:(fc + 1) * 128], hTt[:, k, :],
                       start=(k == 0), stop=(k == 7))
                for k in range(8):
                    MM(P, hkeys + WuK(fc * 128, (fc + 1) * 128), [uk], u_[:], Wu[:, k, fc * 128:(fc + 1) * 128], hTt[:, k, :],
                       start=(k == 0), stop=(k == 7))
                s_, sk = sg.get(m)
                ACT(P, [gk], [sk], s_[:], g_[:], AF.Silu)
                a_, ak = ao.get(m)
                TT(P, [sk, uk], [ak], a_[:], s_[:], u_[:], ALU.mult)
                P.dma("pool", D["ACTD"][fc * 128:(fc + 1) * 128, tok0:tok0 + 512], a_[:], reads=[ak])
                m += 1
                if st + 1 < nst:
                    if fc in (1, 6, 11, 16):
                        prep1(st + 1, (fc - 1) // 5)
                    if fc in (4, 9, 14, 19):
                        prep2(st + 1, (fc - 4) // 5)
        P.flush()


def phase_E2(P, nc, D):
    with ExitStack() as ph:
        sb, ps = allocators(nc, ph)
        wst = Ring(sb, "wst", [128, 512], F32, 4)
        Wd, WdK = load_scaled_weight(P, sb, "Wd", D["w_down"], 22, 1024, None, None, wst, 512)
        gfin = sb("gfin", [128, 1024], F32)
        eps = sb("eps", [128, 1], F32)
        P.dma("sp", gfin[:], D["g_fin"], writes=["gfin"])
        MS(P, ["eps"], eps[:], EPS)
        ar = Ring(sb, "ar", [128, 22, 512], BF16, 2)
        xr = Ring(sb, "xt", [128, 1024], F32, 3)
        po = Ring(ps, "po", [128, 1024], F32, 2)
        yy = Ring(sb, "yy", [128, 1024], F32, 3)
        junk = sb("junk", [128, 1024], BF16)
        ssq = Ring(sb, "ssq", [128, 1], F32, 4)
        for st in range(NT // 512):
            tok0 = st * 512
            a_, ak = ar.get(st)
            P.dma("sp", a_[:], D["ACTD"][:, tok0:tok0 + 512].rearrange("(c p) t -> p c t", p=128), writes=[ak])
            for j in range(4):
                ti = st * 4 + j
                xt, xk = xr.get(ti)
                P.dma("sp", xt[:], D["X2"][tok0 + j * 128:tok0 + (j + 1) * 128, :], writes=[xk])
                o_, ok = po.get(ti)
                for hf in range(2):
                    for c in range(22):
                        MM(P, [ak] + WdK(hf * 512, (hf + 1) * 512), [ok], o_[:, hf * 512:(hf + 1) * 512], a_[:, c, j * 128:(j + 1) * 128],
                           Wd[:, c, hf * 512:(hf + 1) * 512], start=(c == 0), stop=(c == 21))
                y_, yk = yy.get(ti)
                TT(P, [ok, xk], [yk], y_[:], o_[:], xt[:], ALU.add)
                sq, sqk = ssq.get(ti)
                ACT(P, [yk], [sqk], junk[:], y_[:], AF.Square, accum_out=sq[:])
                ACT(P, [sqk, "eps"], [sqk], sq[:], sq[:], AF.Ln, bias=eps[:], scale=1.0 / 1024)
                ACT(P, [sqk], [sqk], sq[:], sq[:], AF.Exp, scale=-0.5)
                STT(P, [yk, sqk, "gfin"], [yk], y_[:], y_[:], sq[:], gfin[:], ALU.mult, ALU.mult)
                P.dma("pool", D["out"][tok0 + j * 128:tok0 + (j + 1) * 128, :], y_[:], reads=[yk])
        P.flush()


INPUT_SPECS = None


def input_specs():
    sp = {
        "xs": ([NT, 1024], F32), "w_in": ([1024, 2432], F32), "g_mix": ([128, 8], F32),
        "cw": ([128, 4, 12], F32), "fw1": ([33, 64], F32), "fb1": ([64, 1], F32), "fw2": ([64, 64], F32),
        "fb2": ([64, 1], F32), "fw3": ([64, 1024], F32), "ffr": ([64, 1], F32), "fdec": ([128, 8], F32),
        "fd": ([128, 4], F32), "gqk": ([128, 2], F32), "g_out": ([128, 8], F32), "w_out": ([1024, 1024], F32),
        "g_ffn": ([128, 8], F32), "w_gate": ([1024, 2816], F32), "w_up": ([1024, 2816], F32),
        "w_down": ([2816, 1024], F32), "g_fin": ([128, 1024], F32),
        "ident": ([128, 128], BF16), "pmat": ([128, 128], F32), "bones": ([128, 128], BF16),
        "ones512": ([128, 128], BF16), "sel": ([65, 64], F32),
        "ropec": ([128, 8192], F32), "ropes": ([128, 8192], F32),
        "c2": ([128, 128], BF16), "s2": ([128, 128], BF16), "s2n": ([128, 128], BF16),
        "cs2": ([128, 256], BF16), "sc2": ([128, 256], BF16),
    }
    for li, (L, N1, nblk, G) in enumerate(LENS):
        s = "_%d" % li
        sp["zt%d" % li] = ([33, L], F32)
        sp["trow%d" % li] = ([128, L], F32)
        sp["f1cat" + s] = ([nblk, 2 * N1], BF16)
        sp["tw" + s] = ([128, 2, G * N1], F32)
        sp["twT" + s] = ([128, 2, 512], F32)
        sp["c1i" + s] = ([128, 64], BF16)
        sp["s1ni" + s] = ([128, 64], BF16)
    return sp


def scratch_specs():
    sp = {
        "U": ([1536, NT], BF16), "QT": ([512, NT], BF16), "KT": ([256, NT], BF16), "VA": ([NT, 130], BF16),
        "Z": ([512, NT], BF16), "X0": ([512, NT], BF16), "YC": ([512, NT], F32), "YATT": ([512, NT], F32),
        "X2": ([NT, 1024], F32), "ACTD": ([2816, NT], BF16),
    }
    for li, (L, N1, nblk, G) in enumerate(LENS):
        sp["KRAW%d" % li] = ([1024, L], F32)
        sp["KF%d" % li] = ([1024, L], BF16)
        sp["RN%d" % li] = ([128, 4], F32)
        sp["KSPEC%d" % li] = ([128, 512, 2, N1], F32)
    return sp


def build(debug=(), phases=None):
    nc = bass.Bass("TRN2", target_bir_lowering=False)
    D = {}
    for name, (shape, dt) in input_specs().items():
        D[name] = nc.dram_tensor(name, list(shape), dt, kind="ExternalInput").ap()
    for name, (shape, dt) in scratch_specs().items():
        kind = "ExternalOutput" if name in debug else "Internal"
        D[name] = nc.dram_tensor(name, list(shape), dt, kind=kind).ap()
    D["out"] = nc.dram_tensor("out", [NT, 1024], F32, kind="ExternalOutput").ap()
    allp = ["A", "F1", "F2", "F3", "B1", "B2", "C", "D", "E1", "E2"]
    phases = allp if phases is None else phases
    with ExitStack() as es:
        P = Prog(nc, es)
        if "A" in phases:
            phase_A(P, nc, D)
        for li in range(2):
            if "F1" in phases:
                phase_F1(P, nc, D, li)
        fused = ("C" in phases and "B1" in phases and "F2" in phases and FUSE_DVE_WORK)
        if fused:
            def extra(sb):
                u1 = b1_units(P, nc, D, sb)
                u2 = f2_units(P, nc, D, 0, sb) + f2_units(P, nc, D, 1, sb)
                out = []
                while u1 or u2:
                    if u2:
                        out.append(u2.pop(0))
                    if u1 and (len(u1) * 5 >= len(u2) * 3 or not u2):
                        out.append(u1.pop(0))
                return out
            phase_C(P, nc, D, [0, 1, 2], extra=extra)
        else:
            for li in range(2):
                if "F2" in phases:
                    phase_F2(P, nc, D, li)
            if "B1" in phases:
                phase_B1(P, nc, D)
        for li in range(2):
            if "F3" in phases:
                phase_F3(P, nc, D, li)
        for si in range(3):
            if "B2" in phases:
                phase_B2(P, nc, D, si)
        if "C" in phases and not fused:
            phase_C(P, nc, D, [0, 1, 2])
        if "D" in phases:
            phase_D(P, nc, D)
        if "E1" in phases:
            phase_E1(P, nc, D)
        if "E2" in phases:
            phase_E2(P, nc, D)
    return nc


def _bf(a):
    return np.ascontiguousarray(a).astype(ml_dtypes.bfloat16)


def host_consts():
    c = {}
    c["ident"] = _bf(np.eye(128))
    pm = np.zeros((128, 128), np.float32)
    for i in range(64):
        pm[2 * i + 1, 2 * i] = -1.0
        pm[2 * i, 2 * i + 1] = 1.0
    c["pmat"] = pm
    bo = np.zeros((128, 128), np.float32)
    bo[:64, :64] = 1.0 / 64
    bo[64:, 64:] = 1.0 / 64
    c["bones"] = _bf(bo)
    c["ones512"] = _bf(np.full((128, 128), 1.0 / 512))
    sel = np.zeros((65, 64), np.float32)
    sel[64, :] = 1.0
    c["sel"] = sel
    t = np.arange(8192)
    row = (t // 64).astype(np.float32)
    col = (t % 64).astype(np.float32)
    inv = (10000.0 ** (-np.arange(0, 32, 2, dtype=np.float32) / 32)).astype(np.float32)
    ang = np.concatenate([row[:, None] * inv, col[:, None] * inv], axis=-1).astype(np.float32)
    cosT = np.cos(ang.astype(np.float64)).T
    sinT = np.sin(ang.astype(np.float64)).T
    idx = (np.arange(128) % 64) // 2
    c["ropec"] = np.ascontiguousarray(cosT[idx]).astype(np.float32)
    c["ropes"] = np.ascontiguousarray(sinT[idx]).astype(np.float32)
    s2 = np.arange(128)
    a2 = 2 * np.pi * np.outer(s2, s2) / 128
    C2, S2 = np.cos(a2), np.sin(a2)
    c["c2"], c["s2"], c["s2n"] = _bf(C2), _bf(S2), _bf(-S2)
    c["cs2"] = _bf(np.concatenate([C2, S2], 1))
    c["sc2"] = _bf(np.concatenate([-S2, C2], 1))
    for li, (L, N1, nblk, G) in enumerate(LENS):
        s = "_%d" % li
        N = 128 * N1
        tt = np.linspace(0.0, 1.0, L, dtype=np.float32)
        w = (np.float32(2.0 * np.pi / L) * np.arange(L, dtype=np.float32)).astype(np.float32)
        bands = np.linspace(1e-4, 15, 16, dtype=np.float32)
        angf = (w[:, None] * bands[None, :]).astype(np.float32)
        z = np.concatenate([tt[:, None], np.cos(angf.astype(np.float64)), -np.sin(angf.astype(np.float64))], -1)
        c["zt%d" % li] = np.ascontiguousarray(z.T).astype(np.float32)
        c["trow%d" % li] = np.ascontiguousarray(np.broadcast_to(tt[None, :], (128, L))).astype(np.float32)
        s1 = np.arange(nblk)
        f1 = np.arange(N1)
        a1 = 2 * np.pi * np.outer(s1, f1) / N1
        c["f1cat" + s] = _bf(np.concatenate([np.cos(a1), -np.sin(a1)], 1))
        aw = 2 * np.pi * np.outer(s2, f1) / N
        tw = np.stack([np.tile(np.cos(aw), (1, G)), np.tile(-np.sin(aw), (1, G))], 1)
        c["tw" + s] = tw.astype(np.float32)
        awT = aw.T
        rep = 128 // N1
        twT = np.stack([np.tile(np.cos(awT), (rep, 4)), np.tile(np.sin(awT), (rep, 4))], 1)
        c["twT" + s] = twT.astype(np.float32)
        c1 = np.zeros((128, 64))
        s1 = np.zeros((128, 64))
        for r_ in range(rep):
            c1[r_ * N1:(r_ + 1) * N1, r_ * nblk:(r_ + 1) * nblk] = np.cos(a1.T) / N
            s1[r_ * N1:(r_ + 1) * N1, r_ * nblk:(r_ + 1) * nblk] = -np.sin(a1.T) / N
        c["c1i" + s] = _bf(c1)
        c["s1ni" + s] = _bf(s1)
    return c


def host_weights(inp):
    f = lambda k: np.asarray(inp[k], np.float32)
    w = {}
    win = f("w_in")[0]
    kc = win[:, 2048:2176]
    w["w_in"] = np.ascontiguousarray(np.concatenate(
        [win[:, :2048], kc[:, :64], kc[:, :64], kc[:, 64:], kc[:, 64:], win[:, 2176:2304]], 1))
    pk = lambda v: np.ascontiguousarray(v.reshape(-1, 128).T)
    w["g_mix"] = pk(f("norm_mix_g")[0])
    cwb = np.concatenate([f("hy_conv_w")[0], f("hy_conv_b")], 0)
    w["cw"] = np.ascontiguousarray(cwb.reshape(4, 12, 128).transpose(2, 0, 1))
    w["fw1"] = f("hy_f_w1")[0]
    w["fb1"] = f("hy_f_b1")[0].reshape(64, 1)
    w["fw2"] = f("hy_f_w2")[0]
    w["fb2"] = f("hy_f_b2")[0].reshape(64, 1)
    w["fw3"] = f("hy_f_w3")[0]
    w["ffr"] = f("hy_f_freq")[0].reshape(64, 1)
    w["fdec"] = np.ascontiguousarray(f("hy_decay")[0].reshape(2, 4, 128).transpose(2, 0, 1).reshape(128, 8))
    w["fd"] = pk(f("hy_d")[0])
    w["gqk"] = np.ascontiguousarray(np.stack([np.tile(f("q_norm_g")[0], 2), np.tile(f("k_norm_g")[0], 2)], 1))
    w["g_out"] = pk(np.concatenate([f("hy_out_g")[0], f("att_out_g")[0]]))
    w["w_out"] = f("w_out")[0]
    w["g_ffn"] = pk(f("norm_ffn_g")[0])
    w["w_gate"] = f("w_gate")[0]
    w["w_up"] = f("w_up")[0]
    w["w_down"] = f("w_down")[0]
    w["g_fin"] = np.ascontiguousarray(np.broadcast_to(f("final_norm_g")[None, :], (128, 1024)))
    return {k: np.ascontiguousarray(v, dtype=np.float32) for k, v in w.items()}


def make_in_maps(inp, cores=range(8)):
    shared = dict(host_consts())
    shared.update(host_weights(inp))
    xp = np.asarray(inp["x_prompt"], np.float32)
    xsm = np.asarray(inp["x_sample"], np.float32)
    maps = []
    for i in cores:
        m = dict(shared)
        m["xs"] = np.ascontiguousarray(np.concatenate([xsm[i], xp[2 * i], xp[2 * i + 1]], 0))
        maps.append(m)
    return maps


_NC_CACHE = {}


def kernel(**inputs):
    if "nc" not in _NC_CACHE:
        _NC_CACHE["nc"] = build()
    nc = _NC_CACHE["nc"]
    maps = make_in_maps(inputs)
    res = run_bass_kernel_spmd(nc, maps, core_ids=list(range(8)))
    y_prompt = np.empty((16, 2048, 1024), np.float32)
    y_sample = np.empty((8, 8192, 1024), np.float32)
    for i in range(8):
        o = np.asarray(res.results[i]["out"], np.float32)
        y_sample[i] = o[:8192]
        y_prompt[2 * i] = o[8192:10240]
        y_prompt[2 * i + 1] = o[10240:12288]
    return (y_prompt, y_sample)
```

```python
from contextlib import ExitStack
import numpy as np
import ml_dtypes
import concourse.bass as bass
import concourse.mybir as mybir
from concourse.bass_utils import run_bass_kernel_spmd

F32 = mybir.dt.float32
BF16 = mybir.dt.bfloat16
I32 = mybir.dt.int32
ALU = mybir.AluOpType
AF = mybir.ActivationFunctionType
AX = mybir.AxisListType

ENGS = ["pe", "act", "dve", "pool", "sp"]
DMA_RING = {"sp": 12, "act": 8, "pool": 10}

NT = 12288
SEQS = [(0, 8192, 0), (8192, 2048, 1), (10240, 2048, 1)]
LENS = [(8192, 128, 64, 4), (2048, 32, 16, 16)]
EPS = 1e-6
TWO_PI = float(2 * np.pi)
DVE_PATH = False
FFT_STORE_Q = "act"
FUSE_DVE_WORK = True


class Op:
    __slots__ = ("eng", "fn", "reads", "writes", "dma", "deps", "ticket", "sem", "idx", "has_dep")


class Prog:
    def __init__(self, nc, es):
        self.nc = nc
        self.ops = []
        self.sem = {e: es.enter_context(nc.semaphore("s_" + e)) for e in ENGS}
        self.bar = es.enter_context(nc.semaphore("s_bar"))
        self.ring = {q: [es.enter_context(nc.semaphore("r_%s%d" % (q, i))) for i in range(n)]
                     for q, n in DMA_RING.items()}
        self.ticket = {e: 0 for e in ENGS}
        self.dma_n = {q: 0 for q in DMA_RING}
        self.known = {e: {} for e in ENGS}
        self.bar_n = 0

    def add(self, eng, fn, reads=(), writes=(), dma=False):
        op = Op()
        op.eng, op.fn, op.dma = eng, fn, dma
        op.reads = tuple(reads)
        op.writes = tuple(writes)
        op.idx = len(self.ops)
        op.has_dep = False
        op.ticket = None
        self.ops.append(op)
        return op

    def dma(self, q, out, in_, reads=(), writes=()):
        return self.add(q, lambda e: e.dma_start(out=out, in_=in_), reads, writes, dma=True)

    def flush(self):
        nc = self.nc
        ops = self.ops
        last_w, readers = {}, {}
        for op in ops:
            deps = set()
            for b in op.reads:
                w = last_w.get(b)
                if w is not None:
                    deps.add(w)
            for b in op.writes:
                w = last_w.get(b)
                if w is not None:
                    deps.add(w)
                for r in readers.get(b, ()):
                    deps.add(r)
            for b in op.reads:
                readers.setdefault(b, []).append(op)
            for b in op.writes:
                last_w[b] = op
                readers[b] = []
            deps.discard(op)
            if op.eng == "pe" and not op.dma:
                deps = {d for d in deps if not (d.eng == "pe" and not d.dma)}
            op.deps = deps
            for d in deps:
                d.has_dep = True
        per_eng = {e: [] for e in ENGS}
        for op in ops:
            per_eng[op.eng].append(op)
        for e in ENGS:
            lst = per_eng[e]
            comp = [o for o in lst if not o.dma]
            if comp:
                comp[-1].has_dep = True
            for o in lst:
                if o.dma:
                    n = self.dma_n[e]
                    R = len(self.ring[e])
                    o.sem = (self.ring[e][n % R], 16 * (n // R + 1))
                    self.dma_n[e] = n + 1
                elif o.has_dep:
                    self.ticket[e] += 1
                    o.ticket = self.ticket[e]
        self.bar_n += len(ENGS)
        bar_target = self.bar_n
        final_ticket = dict(self.ticket)
        prog = self

        def wait(engname, eng, sem, val):
            k = prog.known[engname]
            key = id(sem)
            if k.get(key, 0) >= val:
                return
            eng.wait_ge(sem, val)
            k[key] = val

        def emit(engname):
            def body(eng):
                for o in per_eng[engname]:
                    need = {}
                    for d in o.deps:
                        sm, val = (d.sem[0], d.sem[1]) if d.dma else (prog.sem[d.eng], d.ticket)
                        cur = need.get(id(sm))
                        if cur is None or cur[1] < val:
                            need[id(sm)] = (sm, val)
                    for sm, val in need.values():
                        wait(engname, eng, sm, val)
                    if o.dma:
                        sem, val = o.sem
                        if val > 16:
                            wait(engname, eng, sem, val - 16)
                        o.fn(eng).then_inc(sem, 16)
                    else:
                        ins = o.fn(eng)
                        if o.ticket is not None:
                            ins.then_inc(prog.sem[engname], 1)
                if engname in prog.ring:
                    n = prog.dma_n[engname]
                    R = len(prog.ring[engname])
                    for j in range(max(0, n - R), n):
                        wait(engname, eng, prog.ring[engname][j % R], 16 * (j // R + 1))
                if final_ticket[engname] > 0:
                    wait(engname, eng, prog.sem[engname], final_ticket[engname])
                eng.sem_inc(prog.bar, 1)
                eng.wait_ge(prog.bar, bar_target)
            return body

        with nc.Block() as block:
            block.tensor(emit("pe"))
            block.scalar(emit("act"))
            block.vector(emit("dve"))
            block.gpsimd(emit("pool"))
            block.sync(emit("sp"))
        self.ops = []


def ACT(P, r, w, out, in_, func, **kw):
    P.add("act", lambda e: e.activation(out=out, in_=in_, func=func, **kw), r, w)


def TT(P, r, w, out, a, b, op, eng="dve"):
    P.add(eng, lambda e: e.tensor_tensor(out=out, in0=a, in1=b, op=op), r, w)


def TS(P, r, w, out, a, s1, s2, op0, op1=None, eng="dve"):
    if op1 is None:
        P.add(eng, lambda e: e.tensor_scalar(out=out, in0=a, scalar1=s1, scalar2=None, op0=op0), r, w)
    else:
        P.add(eng, lambda e: e.tensor_scalar(out=out, in0=a, scalar1=s1, scalar2=s2, op0=op0, op1=op1), r, w)


def STT(P, r, w, out, a, s, b, op0, op1, eng="dve"):
    P.add(eng, lambda e: e.scalar_tensor_tensor(out=out, in0=a, scalar=s, in1=b, op0=op0, op1=op1), r, w)


def CP(P, r, w, out, in_, eng="dve"):
    P.add(eng, lambda e: e.tensor_copy(out=out, in_=in_), r, w)


def RCP(P, r, w, out, in_):
    P.add("dve", lambda e: e.reciprocal(out=out, in_=in_), r, w)


def MM(P, r, w, out, lhsT, rhs, start=True, stop=True):
    P.add("pe", lambda e: e.matmul(out=out, lhsT=lhsT, rhs=rhs, start=start, stop=stop), r, w)


def TR(P, r, w, out, in_, ident):
    P.add("pe", lambda e: e.transpose(out=out, in_=in_, identity=ident), r, w)


def MS(P, w, ap, val, eng="pool"):
    P.add(eng, lambda e: e.memset(ap, val), (), w)


class Ring:
    def __init__(self, alloc, name, shape, dt, n):
        self.t = [alloc("%s%d" % (name, i), shape, dt) for i in range(n)]
        self.n = n
        self.name = name

    def get(self, i):
        return self.t[i % self.n], "%s%d" % (self.name, i % self.n)


_PHASE_N = [0]


def allocators(nc, ph):
    _PHASE_N[0] += 1
    pre = "p%d_" % _PHASE_N[0]
    sb = lambda name, shape, dt: ph.enter_context(nc.sbuf_tensor(pre + name, list(shape), dt))
    ps = lambda name, shape, dt: ph.enter_context(nc.psum_tensor(pre + name, list(shape), dt))
    return sb, ps


def seq_of(tok0):
    for si, (off, L, li) in enumerate(SEQS):
        if off <= tok0 < off + L:
            return si, off, L, li
    raise ValueError


class WeightKeys:
    def __init__(self, name, cbw, ncb):
        self.name, self.cbw, self.ncb = name, cbw, ncb

    def __call__(self, c0, c1):
        out = []
        for cb in range(self.ncb):
            if cb * self.cbw < c1 and (cb + 1) * self.cbw > c0:
                out += ["%s_c%d" % (self.name, cb), "%s_c%da" % (self.name, cb)]
        return out


def load_scaled_weight(P, sb, name, dram, nk, cols, gt, gk, wst, cbw, q="sp"):
    W = sb(name, [128, nk, cols], BF16)
    ncb = (cols + cbw - 1) // cbw
    n = 0
    for cb in range(ncb):
        c0, c1 = cb * cbw, min(cols, (cb + 1) * cbw)
        key = "%s_c%d" % (name, cb)
        for k in range(nk):
            st, stk = wst.get(n)
            n += 1
            P.dma(q, st[:, 0:c1 - c0], dram[k * 128:(k + 1) * 128, c0:c1], writes=[stk])
            on_act = (n % 2 == 0)
            if gt is None:
                if on_act:
                    ACT(P, [stk], [key + ("a" if on_act else "")], W[:, k, c0:c1], st[:, 0:c1 - c0], AF.Copy)
                else:
                    CP(P, [stk], [key], W[:, k, c0:c1], st[:, 0:c1 - c0])
            else:
                if on_act:
                    ACT(P, [stk, gk], [key + "a"], W[:, k, c0:c1], st[:, 0:c1 - c0], AF.Copy, scale=gt[:, k:k + 1])
                else:
                    TS(P, [stk, gk], [key], W[:, k, c0:c1], st[:, 0:c1 - c0], gt[:, k:k + 1], None, ALU.mult)
    return W, WeightKeys(name, cbw, ncb)


def norm_part1(P, xt, xk, ti, rings, eps):
    ssq, xnr, pTr, junk = rings
    sq, sqk = ssq.get(ti)
    ACT(P, [xk], [sqk], junk[:], xt[:], AF.Square, accum_out=sq[:])
    ACT(P, [sqk, "eps"], [sqk], sq[:], sq[:], AF.Ln, bias=eps[:], scale=1.0 / 1024)
    ACT(P, [sqk], [sqk], sq[:], sq[:], AF.Exp, scale=-0.5)
    xn, xnk = xnr.get(ti)
    TS(P, [xk, sqk], [xnk], xn[:], xt[:], sq[:], None, ALU.mult)


def norm_part2(P, ti, j, rings, ident, hTt, hTk):
    ssq, xnr, pTr, junk = rings
    xn, xnk = xnr.get(ti)
    pT, pTk = pTr.get(ti)
    for k in range(8):
        TR(P, [xnk, "ident"], [pTk], pT[:, k, :], xn[:, k * 128:(k + 1) * 128], ident[:])
    CP(P, [pTk], ["%s_%d" % (hTk, j)], hTt[:, :, j * 128:(j + 1) * 128], pT[:])


class Pending:
    def __init__(self):
        self.items = []
        self.n = 0

    def add(self, delay, fn):
        self.items.append((self.n + delay, fn))

    def tick(self):
        self.n += 1
        due = [p for p in self.items if p[0] <= self.n]
        self.items = [p for p in self.items if p[0] > self.n]
        for p in due:
            p[1]()

    def drain(self):
        while self.items:
            self.tick()


def phase_A(P, nc, D):
    with ExitStack() as ph:
        sb, ps = allocators(nc, ph)
        wst = Ring(sb, "wst", [128, 608], F32, 4)
        gmix = sb("gmix", [128, 8], F32)
        P.dma("sp", gmix[:], D["g_mix"], writes=["gmix"])
        ident = sb("ident", [128, 128], BF16)
        pmat = sb("pmat", [128, 128], F32)
        bones = sb("bones", [128, 128], BF16)
        gq = sb("gq", [128, 2], F32)
        eps = sb("eps", [128, 1], F32)
        P.dma("sp", ident[:], D["ident"], writes=["ident"])
        P.dma("sp", pmat[:], D["pmat"], writes=["pmat"])
        P.dma("sp", bones[:], D["bones"], writes=["bones"])
        P.dma("sp", gq[:], D["gqk"], writes=["gq"])
        MS(P, ["eps"], eps[:], EPS)
        xr = Ring(sb, "xt", [128, 1024], F32, 3)
        junk = sb("junk", [128, 1024], BF16)
        ssq = Ring(sb, "ssq", [128, 1], F32, 4)
        xnr = Ring(sb, "xn", [128, 1024], BF16, 2)
        pTr = Ring(ps, "pT", [128, 8, 128], BF16, 1)
        hT = Ring(sb, "hT", [128, 8, 512], BF16, 2)
        cs = Ring(sb, "cs", [128, 2, 512], F32, 2)
        pin = Ring(ps, "pin", [128, 512], F32, 4)
        pss = ps("pss", [128, 512], F32)
        ppq = ps("ppq", [128, 512], F32)
        pv = ps("pv", [128, 512], F32)
        hst = Ring(sb, "hst", [128, 512], BF16, 4)
        qs = Ring(sb, "qs", [128, 512], F32, 3)
        qsq = Ring(sb, "qsq", [128, 512], BF16, 3)
        rq = Ring(sb, "rq", [128, 512], F32, 3)
        qg = Ring(sb, "qg", [128, 512], F32, 3)
        t1 = Ring(sb, "t1", [128, 512], F32, 3)
        t2 = Ring(sb, "t2", [128, 512], F32, 3)
        rot = Ring(sb, "rot", [128, 512], BF16, 4)
        vst = Ring(sb, "vst", [128, 2, 65], BF16, 2)
        for i in range(2):
            MS(P, ["vst%d" % i], vst.t[i][:], 1.0)
        rings = (ssq, xnr, pTr, junk)
        nst = NT // 512
        pend = Pending()
        qcnt = [0]

        def prep1(st, j):
            tok0 = st * 512
            ti = st * 4 + j
            xt, xk = xr.get(ti)
            P.dma("sp", xt[:], D["xs"][tok0 + j * 128:tok0 + (j + 1) * 128, :], writes=[xk])
            norm_part1(P, xt, xk, ti, rings, eps)

        def prep2(st, j):
            hTt, hTk = hT.get(st)
            norm_part2(P, st * 4 + j, j, rings, ident, hTt, hTk)

        def cs_load(st):
            tok0 = st * 512
            si, off, L, li = seq_of(tok0)
            tl = tok0 - off
            cst, csk = cs.get(st)
            P.dma("sp", cst[:, 0, :], D["ropec"][:, tl:tl + 512], writes=[csk + "c"])
            P.dma("sp", cst[:, 1, :], D["ropes"][:, tl:tl + 512], writes=[csk + "s"])

        def qk_post1(qn, gcol):
            a, ak = qs.get(qn)
            b, bk = qsq.get(qn)
            MM(P, [bk, "bones"], ["pss"], pss[:], bones[:], b[:])
            r_, rk = rq.get(qn)
            ACT(P, ["pss", "eps"], [rk], r_[:], pss[:], AF.Ln, bias=eps[:], scale=1.0)
            ACT(P, [rk], [rk], r_[:], r_[:], AF.Exp, scale=-0.5)
            g_, gk_ = qg.get(qn)
            STT(P, [ak, rk, "gq"], [gk_], g_[:], a[:], gcol, r_[:], ALU.mult, ALU.mult)

        def qk_post2(qn, st, qi):
            tok0 = st * 512
            cst, csk = cs.get(st)
            g_, gk_ = qg.get(qn)
            MM(P, [gk_, "pmat"], ["ppq"], ppq[:], pmat[:], g_[:])
            u1, u1k = t1.get(qn)
            u2, u2k = t2.get(qn)
            TT(P, [gk_, csk + "c"], [u1k], u1[:], g_[:], cst[:, 0, :], ALU.mult, eng="pool")
            TT(P, ["ppq", csk + "s"], [u2k], u2[:], ppq[:], cst[:, 1, :], ALU.mult)
            ro, rok = rot.get(qn)
            TT(P, [u1k, u2k], [rok], ro[:], u1[:], u2[:], ALU.add, eng="pool")
            if qi < 4:
                P.dma("pool", D["QT"][qi * 128:(qi + 1) * 128, tok0:tok0 + 512], ro[:], reads=[rok])
            else:
                P.dma("pool", D["KT"][(qi - 4) * 128:(qi - 3) * 128, tok0:tok0 + 512], ro[:], reads=[rok])

        cs_load(0)
        for j in range(4):
            prep1(0, j)
            prep2(0, j)
        Wsb, WsbK = load_scaled_weight(P, sb, "Wsb", D["w_in"], 8, 2432, gmix, "gmix", wst, 608)
        for st in range(nst):
            tok0 = st * 512
            hTt, hTk = hT.get(st)
            hkeys = ["%s_%d" % (hTk, j) for j in range(4)]
            for c in range(18):
                pt, pk = pin.get(st * 18 + c)
                for k in range(8):
                    MM(P, hkeys + WsbK(c * 128, (c + 1) * 128), [pk], pt[:], Wsb[:, k, c * 128:(c + 1) * 128], hTt[:, k, :],
                       start=(k == 0), stop=(k == 7))
                if c < 12:
                    ht, hk = hst.get(st * 12 + c)
                    if c % 4 != 3:
                        ACT(P, [pk], [hk], ht[:], pt[:], AF.Copy)
                    else:
                        CP(P, [pk], [hk], ht[:], pt[:])
                    P.dma("pool", D["U"][c * 128:(c + 1) * 128, tok0:tok0 + 512], ht[:], reads=[hk])
                else:
                    qi = c - 12
                    gcol = gq[:, 0:1] if qi < 4 else gq[:, 1:2]
                    qn = qcnt[0]
                    qcnt[0] += 1
                    a, ak = qs.get(qn)
                    b, bk = qsq.get(qn)
                    ACT(P, [pk], [ak], a[:], pt[:], AF.Copy)
                    ACT(P, [pk], [bk], b[:], pt[:], AF.Square)
                    pend.add(1, (lambda qn=qn, gcol=gcol: qk_post1(qn, gcol)))
                    pend.add(2, (lambda qn=qn, st=st, qi=qi: qk_post2(qn, st, qi)))
                pend.tick()
                if st + 1 < nst:
                    if c == 0:
                        cs_load(st + 1)
                    if c in (1, 5, 9, 13):
                        prep1(st + 1, (c - 1) // 4)
                    if c in (4, 8, 12, 16):
                        prep2(st + 1, (c - 4) // 4)
            for j in range(4):
                ti = st * 4 + j
                for k in range(8):
                    MM(P, [hkeys[j]] + WsbK(2304, 2432), ["pv"], pv[:, 0:128], hTt[:, k, j * 128:(j + 1) * 128],
                       Wsb[:, k, 2304:2432], start=(k == 0), stop=(k == 7))
                vt, vk = vst.get(ti)
                CP(P, ["pv"], [vk], vt[:, :, 0:64], pv[:, 0:128].rearrange("p (g d) -> p g d", g=2))
                P.dma("pool", D["VA"][tok0 + j * 128:tok0 + (j + 1) * 128, :],
                      vt[:].rearrange("p g d -> p (g d)"), reads=[vk])
                pend.tick()
        pend.drain()
        P.flush()


def sin_reduced(P, sb_tiles, n, keys_in, arg, argk, out, outk):
    ki, y = sb_tiles
    kit, kik = ki.get(n)
    yt, yk = y.get(n)
    TS(P, [argk], [kik], kit[:], arg, 1.0 / TWO_PI, None, ALU.mult)
    STT(P, [kik, argk], [yk], yt[:], kit[:], -TWO_PI, arg, ALU.mult, ALU.add)
    TS(P, [yk], [yk], yt[:], yt[:], float(np.pi), -float(np.pi), ALU.min, ALU.max)
    ACT(P, [yk], [outk], out, yt[:], AF.Sin)


def phase_F1(P, nc, D, li):
    L, N1, nblk, G = LENS[li]
    ntc = L // 512
    with ExitStack() as ph:
        sb, ps = allocators(nc, ph)
        w1 = sb("w1", [33, 64], F32)
        w2 = sb("w2", [64, 64], F32)
        w3 = sb("w3", [64, 1024], F32)
        w3b = sb("w3b", [64, 1024], BF16)
        fr = sb("fr", [64, 1], F32)
        b1 = sb("b1", [64, 1], F32)
        b2 = sb("b2", [64, 1], F32)
        ndec = sb("ndec", [128, 8], F32)
        P.dma("sp", w1[:], D["fw1"], writes=["w1"])
        P.dma("sp", w2[:], D["fw2"], writes=["w2"])
        P.dma("sp", w3[:], D["fw3"], writes=["w3f"])
        CP(P, ["w3f"], ["w3"], w3b[:], w3[:])
        P.dma("sp", fr[:], D["ffr"], writes=["fr"])
        P.dma("sp", b1[:], D["fb1"], writes=["b1"])
        P.dma("sp", b2[:], D["fb2"], writes=["b2"])
        P.dma("sp", ndec[:], D["fdec"], writes=["ndec"])
        TT(P, ["b1", "fr"], ["b1"], b1[:], b1[:], fr[:], ALU.mult)
        TT(P, ["b2", "fr"], ["b2"], b2[:], b2[:], fr[:], ALU.mult)
        ACT(P, ["ndec"], ["ndec"], ndec[:], ndec[:], AF.Abs)
        TS(P, ["ndec"], ["ndec"], ndec[:], ndec[:], -1.0, None, ALU.mult)
        zt = Ring(sb, "zt", [33, 512], F32, 4)
        tr = Ring(sb, "tr", [128, 512], F32, 2)
        p1 = Ring(ps, "p1", [64, 512], F32, 2)
        p2 = Ring(ps, "p2", [64, 512], F32, 2)
        p3 = Ring(ps, "p3", [128, 512], F32, 3)
        a1 = Ring(sb, "a1", [64, 512], F32, 4)
        ki = Ring(sb, "ki", [64, 512], I32, 4)
        yy = Ring(sb, "yy", [64, 512], F32, 4)
        h1 = Ring(sb, "h1", [64, 512], F32, 4)
        h2all = sb("h2all", [64, L], BF16)
        win = Ring(sb, "win", [128, 512], F32, 3)
        hk = Ring(sb, "hk", [128, 512], F32, 3)
        junk = sb("junk", [128, 512], F32)
        asum = sb("asum", [128, 8, ntc], F32)
        for t0 in range(0, ntc, 4):
            tcs = list(range(t0, min(t0 + 4, ntc)))
            for tc in tcs:
                z, zk = zt.get(tc)
                P.dma("sp", z[:], D["zt%d" % li][:, tc * 512:(tc + 1) * 512], writes=[zk])
                pp, ppk = p1.get(tc)
                MM(P, [zk, "w1"], [ppk], pp[:], w1[:], z[:])
                a, ak = a1.get(tc)
                ACT(P, [ppk, "fr", "b1"], [ak], a[:], pp[:], AF.Identity, bias=b1[:], scale=fr[:])
            for tc in tcs:
                a, ak = a1.get(tc)
                h, hk1 = h1.get(tc)
                sin_reduced(P, (ki, yy), tc, None, a[:], ak, h[:], hk1)
            for tc in tcs:
                h, hk1 = h1.get(tc)
                pp, ppk = p2.get(tc)
                MM(P, [hk1, "w2"], [ppk], pp[:], w2[:], h[:])
                a, ak = a1.get(tc)
                ACT(P, [ppk, "fr", "b2"], [ak], a[:], pp[:], AF.Identity, bias=b2[:], scale=fr[:])
            for tc in tcs:
                a, ak = a1.get(tc)
                sin_reduced(P, (ki, yy), tc, None, a[:], ak, h2all[:, tc * 512:(tc + 1) * 512], "h2_%d" % tc)
        units2 = [(tc, oc) for tc in range(ntc) for oc in range(8)]

        def emit_win(m):
            tc, oc = units2[m]
            t_, tk = tr.get(tc)
            if oc == 0:
                P.dma("sp", t_[:], D["trow%d" % li][:, tc * 512:(tc + 1) * 512], writes=[tk])
            w_, wk = win.get(m)
            ACT(P, [tk, "ndec"], [wk], w_[:], t_[:], AF.Exp, scale=ndec[:, oc:oc + 1])

        emit_win(0)
        for m, (tc, oc) in enumerate(units2):
            pt, pk = p3.get(m)
            MM(P, ["h2_%d" % tc, "w3"], [pk], pt[:], w3b[:, oc * 128:(oc + 1) * 128], h2all[:, tc * 512:(tc + 1) * 512])
            if m + 1 < len(units2):
                emit_win(m + 1)
            w_, wk = win.get(m)
            k_, kk = hk.get(m)
            TT(P, [pk, wk], [kk], k_[:], pt[:], w_[:], ALU.mult)
            if oc >= 4 and tc == 0:
                MS(P, [kk], k_[:, 0:1], 0.0, eng="dve")
            ACT(P, [kk], ["asum_%d_%d" % (oc, tc)], junk[:], k_[:], AF.Abs, accum_out=asum[:, oc, tc:tc + 1])
            P.dma("pool", D["KRAW%d" % li][oc * 128:(oc + 1) * 128, tc * 512:(tc + 1) * 512], k_[:], reads=[kk])
        tot = sb("tot", [128, 8], F32)
        rn = sb("rn", [128, 4], F32)
        allk = ["asum_%d_%d" % (oc, tc) for oc in range(8) for tc in range(ntc)]
        P.add("dve", lambda e: e.tensor_reduce(out=tot[:], in_=asum[:], axis=AX.X, op=ALU.add), allk, ["tot"])
        TT(P, ["tot"], ["rn"], rn[:], tot[:, 0:4], tot[:, 4:8], ALU.add)
        RCP(P, ["rn"], ["rn"], rn[:], rn[:])
        P.dma("pool", D["RN%d" % li], rn[:], reads=["rn"])
        P.flush()


def f2_units(P, nc, D, li, sb):
    L, N1, nblk, G = LENS[li]
    ntc = L // 512
    pre = "f2_%d_" % li
    rn = sb(pre + "rn", [128, 4], F32)
    hd = sb(pre + "hd", [128, 4], F32)
    P.dma("sp", rn[:], D["RN%d" % li], writes=[pre + "rn"])
    P.dma("sp", hd[:], D["fd"], writes=[pre + "hd"])
    kr = Ring(sb, pre + "kr", [128, 512], F32, 3)
    kb = Ring(sb, pre + "kb", [128, 512], BF16, 3)
    units = []

    def unit(m, oc, tc):
        a, ak = kr.get(m)
        b, bk = kb.get(m)
        P.dma("sp", a[:], D["KRAW%d" % li][oc * 128:(oc + 1) * 128, tc * 512:(tc + 1) * 512], writes=[ak])
        TS(P, [ak, pre + "rn"], [ak], a[:], a[:], rn[:, oc % 4:oc % 4 + 1], None, ALU.mult)
        if oc < 4 and tc == 0:
            TT(P, [ak, pre + "hd"], [ak], a[:, 0:1], a[:, 0:1], hd[:, oc:oc + 1], ALU.add)
        CP(P, [ak], [bk], b[:], a[:])
        P.dma("pool", D["KF%d" % li][oc * 128:(oc + 1) * 128, tc * 512:(tc + 1) * 512], b[:], reads=[bk])

    m = 0
    for oc in range(8):
        for tc in range(ntc):
            units.append(lambda m=m, oc=oc, tc=tc: unit(m, oc, tc))
            m += 1
    return units


def phase_F2(P, nc, D, li):
    with ExitStack() as ph:
        sb, ps = allocators(nc, ph)
        for u in f2_units(P, nc, D, li, sb):
            u()
        P.flush()


class FFTConsts:
    def __init__(self, P, sb, D, li, inverse):
        L, N1, nblk, G = LENS[li]
        s = "_%d" % li
        self.f1cat = sb("f1cat", [nblk, 2 * N1], BF16)
        self.tw = sb("tw", [128, 2, G * N1], F32)
        self.c2 = sb("c2", [128, 128], BF16)
        self.s2 = sb("s2", [128, 128], BF16)
        self.s2n = sb("s2n", [128, 128], BF16)
        P.dma("sp", self.f1cat[:], D["f1cat" + s], writes=["fc"])
        P.dma("sp", self.tw[:], D["tw" + s], writes=["fc"])
        P.dma("sp", self.c2[:], D["c2"], writes=["fc"])
        P.dma("sp", self.s2[:], D["s2"], writes=["fc"])
        P.dma("sp", self.s2n[:], D["s2n"], writes=["fc"])
        if inverse:
            self.cs2 = sb("cs2", [128, 256], BF16)
            self.sc2 = sb("sc2", [128, 256], BF16)
            self.twT = sb("twT", [128, 2, 512], F32)
            self.c1i = sb("c1i", [128, 64], BF16)
            self.s1ni = sb("s1ni", [128, 64], BF16)
            P.dma("sp", self.cs2[:], D["cs2"], writes=["fc"])
            P.dma("sp", self.sc2[:], D["sc2"], writes=["fc"])
            P.dma("sp", self.twT[:], D["twT" + s], writes=["fc"])
            P.dma("sp", self.c1i[:], D["c1i" + s], writes=["fc"])
            P.dma("sp", self.s1ni[:], D["s1ni" + s], writes=["fc"])


def cmul(P, rk, tmp4, n, v3, Xr, Xi, Kr, Ki, outr, outrk, outi, outik):
    (a1, a1k), (b1, b1k), (a2, a2k), (b2, b2k) = [r.get(n) for r in tmp4]
    TT(P, rk, [a1k], v3(a1), Xr, Kr, ALU.mult)
    TT(P, rk, [b1k], v3(b1), Xi, Ki, ALU.mult)
    TT(P, rk, [a2k], v3(a2), Xr, Ki, ALU.mult)
    TT(P, rk, [b2k], v3(b2), Xi, Kr, ALU.mult)
    TT(P, [a1k, b1k], [outrk], outr, a1[:], b1[:], ALU.subtract, eng="pool")
    TT(P, [a2k, b2k], [outik], outi, a2[:], b2[:], ALU.add, eng="pool")


def tmp4(sb, name):
    return [Ring(sb, "%s%d" % (name, i), [128, 512], F32, 2) for i in range(4)]


def fft_P1(P, C, li, zb, zbk, psA, pk="psA"):
    L, N1, nblk, G = LENS[li]
    for ch in range(G):
        MM(P, [zbk, "fc"], [pk], psA[:, ch * 2 * N1:(ch + 1) * 2 * N1], zb[:, ch, :], C.f1cat[:])


def fft_D2(P, C, li, psA, t4, n, rp, rpk, ip, ipk, pk="psA"):
    L, N1, nblk, G = LENS[li]
    Av = psA[:].rearrange("p (g r f) -> p g r f", g=G, r=2)
    v3 = lambda t: t[:].rearrange("p (g f) -> p g f", g=G)
    Wr = C.tw[:, 0, :].rearrange("p (g f) -> p g f", g=G)
    Wi = C.tw[:, 1, :].rearrange("p (g f) -> p g f", g=G)
    cmul(P, [pk, "fc"], t4, n, v3, Av[:, :, 0, :], Av[:, :, 1, :], Wr, Wi, rp[:], rpk, ip[:], ipk)


def fft_P3(P, C, rp, rpk, ip, ipk, psX, xk="psX"):
    MM(P, [rpk, "fc"], [xk], psX[:, 0, :], C.c2[:], rp[:], start=True, stop=False)
    MM(P, [ipk, "fc"], [xk], psX[:, 0, :], C.s2[:], ip[:], start=False, stop=True)
    MM(P, [rpk, "fc"], [xk], psX[:, 1, :], C.s2n[:], rp[:], start=True, stop=False)
    MM(P, [ipk, "fc"], [xk], psX[:, 1, :], C.c2[:], ip[:], start=False, stop=True)


def phase_F3(P, nc, D, li):
    L, N1, nblk, G = LENS[li]
    GN = G * N1
    with ExitStack() as ph:
        sb, ps = allocators(nc, ph)
        C = FFTConsts(P, sb, D, li, inverse=False)
        zbr = Ring(sb, "zb", [nblk, G, 128], BF16, 4)
        psA = Ring(ps, "psA", [128, 2 * GN], F32, 2)
        psXr = Ring(ps, "psX", [128, 2, GN], F32, 2)
        t4 = tmp4(sb, "ta")
        arp = Ring(sb, "arp", [128, GN], BF16, 2)
        aip = Ring(sb, "aip", [128, GN], BF16, 2)
        xf = Ring(sb, "xf", [128, 2, GN], F32, 2)
        ko = Ring(sb, "ko", [128, G, 2, N1], F32, 2)
        nu = 2 * (512 // G)

        def load(u):
            gi, d = divmod(u, 2)
            c0 = d * 512 + gi * G
            z, zk = zbr.get(u)
            P.dma("sp", z[:], D["KF%d" % li][c0:c0 + G, :].rearrange("c (b i) -> b c i", i=128), writes=[zk])

        def P1(u):
            z, zk = zbr.get(u)
            a_, ak_ = psA.get(u)
            fft_P1(P, C, li, z, zk, a_, pk=ak_)

        def D2(u):
            rp, rpk = arp.get(u)
            ip, ipk = aip.get(u)
            a_, ak_ = psA.get(u)
            fft_D2(P, C, li, a_, t4, u, rp, rpk, ip, ipk, pk=ak_)

        def P3(u):
            rp, rpk = arp.get(u)
            ip, ipk = aip.get(u)
            x_, xk_ = psXr.get(u)
            fft_P3(P, C, rp, rpk, ip, ipk, x_, xk=xk_)

        def D4(u):
            gi, d = divmod(u, 2)
            psX, pxk = psXr.get(u)
            x_, xk = xf.get(gi)
            if d == 0:
                ACT(P, [pxk], [xk], x_[:], psX[:], AF.Copy)
            else:
                k_, kk = ko.get(gi)
                g3 = lambda a: a.rearrange("p (g f) -> p g f", g=G)
                TT(P, [pxk, xk], [kk + "r"], k_[:, :, 0, :], g3(psX[:, 0, :]), g3(x_[:, 0, :]), ALU.add)
                STT(P, [pxk, xk], [kk + "i"], k_[:, :, 1, :], g3(psX[:, 1, :]), -1.0, g3(x_[:, 1, :]), ALU.mult, ALU.add)
                c0 = gi * G
                P.dma(FFT_STORE_Q, D["KSPEC%d" % li][:, c0:c0 + G, :, :], k_[:], reads=[kk + "r", kk + "i"])

        load(0)
        load(1)
        P1(0)
        for t in range(nu + 2):
            if t + 2 < nu:
                load(t + 2)
            if t < nu:
                D2(t)
            if 0 <= t - 1 < nu:
                P3(t - 1)
                D4(t - 1)
            if t + 1 < nu:
                P1(t + 1)
        P.flush()


def b1_units(P, nc, D, sb):
    cw = sb("b1cw", [128, 4, 12], F32)
    P.dma("sp", cw[:], D["cw"], writes=["b1cw"])
    ub = Ring(sb, "b1ub", [128, 514], BF16, 6)
    ta = Ring(sb, "b1ta", [128, 512], F32, 3)
    uc = Ring(sb, "b1uc", [128, 512], F32, 3)
    zo = Ring(sb, "b1zo", [128, 512], BF16, 3)
    xo = Ring(sb, "b1xo", [128, 512], BF16, 3)

    def unit(q, st, j):
        tok0 = st * 512
        si, off, L, li = seq_of(tok0)
        tl = tok0 - off
        res = []
        for part in range(3):
            n = q * 3 + part
            c = part * 4 + j
            u, uk = ub.get(n)
            lo = 0 if tl > 0 else 1
            hi = 514 if tl + 512 < L else 513
            if lo == 1:
                MS(P, [uk + "l"], u[:, 0:1], 0.0)
            if hi == 513:
                MS(P, [uk + "h"], u[:, 513:514], 0.0)
            P.dma("sp", u[:, lo:hi], D["U"][c * 128:(c + 1) * 128, tok0 - 1 + lo:tok0 - 1 + hi],
                  reads=[uk + "l", uk + "h"], writes=[uk])
            rk = [uk, uk + "l", uk + "h", "b1cw"]
            a, ak = ta.get(n)
            o, ok = uc.get(n)
            TS(P, rk, [ak], a[:], u[:, 0:512], cw[:, 0, c:c + 1], cw[:, 3, c:c + 1], ALU.mult, ALU.add)
            STT(P, rk + [ak], [ak], a[:], u[:, 1:513], cw[:, 1, c:c + 1], a[:], ALU.mult, ALU.add)
            if part == 0:
                xt, xk = xo.get(q)
                STT(P, rk + [ak], [xk], xt[:], u[:, 2:514], cw[:, 2, c:c + 1], a[:], ALU.mult, ALU.add)
                P.dma("pool", D["X0"][j * 128:(j + 1) * 128, tok0:tok0 + 512], xt[:], reads=[xk])
            else:
                STT(P, rk + [ak], [ok], o[:], u[:, 2:514], cw[:, 2, c:c + 1], a[:], ALU.mult, ALU.add)
                res.append((o, ok))
        zt, zk = zo.get(q)
        TT(P, [res[0][1], res[1][1]], [zk], zt[:], res[0][0][:], res[1][0][:], ALU.mult)
        P.dma("pool", D["Z"][j * 128:(j + 1) * 128, tok0:tok0 + 512], zt[:], reads=[zk])

    units = []
    q = 0
    for st in range(NT // 512):
        for j in range(4):
            units.append(lambda q=q, st=st, j=j: unit(q, st, j))
            q += 1
    return units


def phase_B1(P, nc, D):
    with ExitStack() as ph:
        sb, ps = allocators(nc, ph)
        for u in b1_units(P, nc, D, sb):
            u()
        P.flush()


def phase_B2(P, nc, D, si):
    off, L, li = SEQS[si]
    _, N1, nblk, G = LENS[li]
    GN = G * N1
    ng = 512 // G
    with ExitStack() as ph:
        sb, ps = allocators(nc, ph)
        C = FFTConsts(P, sb, D, li, inverse=True)
        zbr = Ring(sb, "zb", [nblk, G, 128], BF16, 4)
        ksp = Ring(sb, "ksp", [128, G, 2, N1], F32, 4)
        psA = ps("psA", [128, 2 * GN], F32)
        psX = ps("psX", [128, 2, GN], F32)
        psB = ps("psB", [128, 4, 2, 128], F32)
        psYr = Ring(ps, "psY", [64, 512], F32, 2)
        ta, tb, tc = tmp4(sb, "ta"), tmp4(sb, "tb"), tmp4(sb, "tc")
        arp = Ring(sb, "arp", [128, GN], BF16, 2)
        aip = Ring(sb, "aip", [128, GN], BF16, 2)
        yr = Ring(sb, "yr", [128, GN], BF16, 2)
        yi = Ring(sb, "yi", [128, GN], BF16, 2)
        brp = Ring(sb, "brp", [128, 512], BF16, 2)
        bip = Ring(sb, "bip", [128, 512], BF16, 2)
        yo = Ring(sb, "yo", [64, 4, 128], F32, 3)

        def load(g):
            c0 = g * G
            z, zk = zbr.get(g)
            P.dma("sp", z[:], D["Z"][c0:c0 + G, off:off + L].rearrange("c (b i) -> b c i", i=128), writes=[zk])
            k_, kk = ksp.get(g)
            P.dma("sp", k_[:], D["KSPEC%d" % li][:, c0:c0 + G, :, :], writes=[kk])

        def P1(g):
            z, zk = zbr.get(g)
            fft_P1(P, C, li, z, zk, psA)

        def D2(g):
            rp, rpk = arp.get(g)
            ip, ipk = aip.get(g)
            fft_D2(P, C, li, psA, ta, g, rp, rpk, ip, ipk)

        def P3(g):
            rp, rpk = arp.get(g)
            ip, ipk = aip.get(g)
            fft_P3(P, C, rp, rpk, ip, ipk, psX)

        def D4(g):
            k_, kk = ksp.get(g)
            v3 = lambda t: t[:].rearrange("p (g f) -> p g f", g=G)
            g3 = lambda a: a.rearrange("p (g f) -> p g f", g=G)
            y_r, yrk = yr.get(g)
            y_i, yik = yi.get(g)
            cmul(P, ["psX", kk], tb, g, v3, g3(psX[:, 0, :]), g3(psX[:, 1, :]), k_[:, :, 0, :], k_[:, :, 1, :],
                 y_r[:], yrk, y_i[:], yik)

        def P5(g):
            y_r, yrk = yr.get(g)
            y_i, yik = yi.get(g)
            for u in range(4):
                lr, l_i = y_r[:, u * 128:(u + 1) * 128], y_i[:, u * 128:(u + 1) * 128]
                ob = psB[:, u, :, :].rearrange("p r s -> p (r s)")
                MM(P, [yrk, "fc"], ["psB"], ob, lr, C.cs2[:], start=True, stop=False)
                MM(P, [yik, "fc"], ["psB"], ob, l_i, C.sc2[:], start=False, stop=True)

        def D6(g):
            v3 = lambda t: t[:].rearrange("p (g s) -> p g s", g=4)
            Tc = C.twT[:, 0, :].rearrange("p (g s) -> p g s", g=4)
            Ts_ = C.twT[:, 1, :].rearrange("p (g s) -> p g s", g=4)
            rp, rpk = brp.get(g)
            ip, ipk = bip.get(g)
            cmul(P, ["psB", "fc"], tc, g, v3, psB[:, :, 0, :], psB[:, :, 1, :], Tc, Ts_, rp[:], rpk, ip[:], ipk)

        def P7(g):
            rp, rpk = brp.get(g)
            ip, ipk = bip.get(g)
            psY, pyk = psYr.get(g)
            for u in range(4):
                MM(P, [rpk, "fc"], [pyk], psY[:, u * 128:(u + 1) * 128], C.c1i[:], rp[:, u * 128:(u + 1) * 128],
                   start=True, stop=False)
                MM(P, [ipk, "fc"], [pyk], psY[:, u * 128:(u + 1) * 128], C.s1ni[:], ip[:, u * 128:(u + 1) * 128],
                   start=False, stop=True)
            y_, yk = yo.get(g)
            ACT(P, [pyk], [yk], y_[:].rearrange("p g s -> p (g s)"), psY[:], AF.Copy)
            c0 = g * G
            if li == 0:
                P.dma(FFT_STORE_Q, D["YC"][c0:c0 + 4, off:off + L].rearrange("c (b i) -> b c i", i=128), y_[:], reads=[yk])
            else:
                ycv = D["YC"][c0:c0 + 16, off:off + L].rearrange("(s c) (b i) -> c b s i", c=4, i=128)
                for c4 in range(4):
                    P.dma(FFT_STORE_Q, ycv[c4], y_[c4 * 16:(c4 + 1) * 16, :, :], reads=[yk])

        load(0)
        load(1)
        P1(0)
        for t in range(ng + 4):
            if t + 2 < ng:
                load(t + 2)
            if t < ng:
                D2(t)
            if 0 <= t - 1 < ng:
                P3(t - 1)
                D4(t - 1)
            if 0 <= t - 2 < ng:
                P5(t - 2)
                D6(t - 2)
            if 0 <= t - 3 < ng:
                P7(t - 3)
            if t + 1 < ng:
                P1(t + 1)
        P.flush()


def phase_C(P, nc, D, sis, extra=None):
    with ExitStack() as ph:
        sb, ps = allocators(nc, ph)
        sel = sb("sel", [65, 64], F32)
        P.dma("sp", sel[:], D["sel"], writes=["sel"])
        qt = Ring(sb, "qt", [128, 512], BF16, 3)
        psS = Ring(ps, "psS", [128, 512], F32, 4)
        psO = [ps("psO%d" % i, [65, 512], F32) for i in range(2)]
        psD = Ring(ps, "psD", [64, 512], F32, 2)
        pt = Ring(sb, "pt", [128, 512], BF16, 6)
        osb = Ring(sb, "osb", [65, 512], F32, 4)
        rden = Ring(sb, "rden", [64, 512], F32, 2)
        on = Ring(sb, "on", [64, 512], F32, 3)
        kvs = {}
        kv_loads = {}
        for si in sis:
            off, L, li = SEQS[si]
            nkb = L // 128
            va2 = sb("va2_%d" % si, [128, nkb, 130], BF16)
            lst = []
            loads = []
            for g in range(2):
                k_ = sb("kT_%d_%d" % (si, g), [128, L], BF16)
                kk = "kT_%d_%d" % (si, g)
                loads.append((lambda k_=k_, kk=kk, g=g, off=off, L=L:
                              P.dma("sp", k_[:], D["KT"][g * 128:(g + 1) * 128, off:off + L], writes=[kk])))
                if g == 0:
                    loads.append((lambda va2=va2, si=si, off=off, L=L:
                                  P.dma("sp", va2[:], D["VA"][off:off + L, :].rearrange("(b p) d -> p b d", p=128),
                                        writes=["va2_%d" % si])))
                lst.append((k_, kk, va2[:, :, g * 65:(g + 1) * 65], "va2_%d" % si))
            kvs[si] = lst
            kv_loads[si] = loads
        for f in kv_loads[sis[0]][:2]:
            f()
        units = extra(sb) if extra is not None else []
        pending = []
        st = {"gi": 0, "n": 0, "ui": 0, "qb": 0}

        for si in sis:
            off, L, li = SEQS[si]
            nkb = L // 128
            nqc = L // 512
            kv = kvs[si]
            blocks = [(g, hp, qc) for g in range(2) for hp in range(2) for qc in range(nqc)]
            iters = [(bi, kb) for bi in range(len(blocks)) for kb in range(nkb)]
            use_units = units if si == sis[0] else []
            every = max(1, len(iters) // max(1, len(use_units))) if use_units else 0
            qtiles = {}
            g0 = st["gi"]
            qb0 = st["qb"]

            def emit_S(i):
                bi, kb = iters[i]
                g, hp, qc = blocks[bi]
                j = g * 2 + hp
                k_, kk, v_, vk = kv[g]
                if kb == 0:
                    q_, qk = qt.get(qb0 + bi)
                    P.dma("sp", q_[:], D["QT"][j * 128:(j + 1) * 128, off + qc * 512:off + (qc + 1) * 512], writes=[qk])
                    qtiles[bi] = (q_, qk)
                q_, qk = qtiles[bi]
                for hh in range(2):
                    r0 = hh * 64
                    s_, sk = psS.get(2 * (g0 + i) + hh)
                    MM(P, [kk, qk], [sk], s_[:], k_[r0:r0 + 64, kb * 128:(kb + 1) * 128], q_[r0:r0 + 64, :])

            def epilogue2(g, hp, qc, hh, ob, obk, n, off=off):
                h = 2 * (g * 2 + hp) + hh
                pd, pdk = psD.get(n)
                MM(P, [obk, "sel"], [pdk], pd[:], sel[:], ob[:])
                rd, rdk = rden.get(n)
                RCP(P, [pdk], [rdk], rd[:], pd[:])
                o2, o2k = on.get(n)
                TT(P, [obk, rdk], [o2k], o2[:], ob[0:64, :], rd[:], ALU.mult, eng="pool")
                P.dma("pool", D["YATT"][h * 64:(h + 1) * 64, off + qc * 512:off + (qc + 1) * 512], o2[:], reads=[o2k])

            emit_S(0)
            if si == sis[0]:
                kv_loads[si][2]()
            for i in range(len(iters)):
                if si == sis[0] and i == 24:
                    for sj in sis[1:]:
                        for f in kv_loads[sj]:
                            f()
                bi, kb = iters[i]
                g, hp, qc = blocks[bi]
                k_, kk, v_, vk = kv[g]
                gi = g0 + i
                if i + 1 < len(iters):
                    emit_S(i + 1)
                sks = [psS.get(2 * gi + hh)[1] for hh in range(2)]
                for hh in range(2):
                    s_, sk = psS.get(2 * gi + hh)
                    p_, pk = pt.get(2 * gi + hh)
                    ACT(P, sks if hh == 0 else [sk], [pk], p_[:], s_[:], AF.Exp, scale=0.125)
                for hh in range(2):
                    p_, pk = pt.get(2 * gi + hh)
                    MM(P, [vk, pk], ["psO%d" % hh], psO[hh][:], v_[:, kb, :], p_[:], start=(kb == 0), stop=(kb == nkb - 1))
                still = []
                for (at, fn) in pending:
                    if at <= gi:
                        fn()
                    else:
                        still.append((at, fn))
                pending = still
                if kb == nkb - 1:
                    for hh in range(2):
                        n = st["n"]
                        ob, obk = osb.get(n)
                        CP(P, ["psO%d" % hh], [obk], ob[:], psO[hh][:])
                        pending.append((gi + 2 + 3 * hh, (lambda g=g, hp=hp, qc=qc, hh=hh, ob=ob, obk=obk, n=n, f=epilogue2:
                                                 f(g, hp, qc, hh, ob, obk, n))))
                        st["n"] += 1
                if use_units and i % every == min(4, every - 1) and st["ui"] < len(use_units):
                    use_units[st["ui"]]()
                    st["ui"] += 1
            st["gi"] += len(iters)
            st["qb"] += len(blocks)
            while use_units and st["ui"] < len(use_units):
                use_units[st["ui"]]()
                st["ui"] += 1
        for (at, fn) in pending:
            fn()
        P.flush()


def phase_D(P, nc, D):
    with ExitStack() as ph:
        sb, ps = allocators(nc, ph)
        wst = Ring(sb, "wst", [128, 512], F32, 4)
        gout = sb("gout", [128, 8], F32)
        P.dma("sp", gout[:], D["g_out"], writes=["gout"])
        Wo, WoK = load_scaled_weight(P, sb, "Wo", D["w_out"], 8, 1024, gout, "gout", wst, 512)
        onesm = sb("onesm", [128, 128], BF16)
        eps = sb("eps", [128, 1], F32)
        P.dma("sp", onesm[:], D["ones512"], writes=["onesm"])
        MS(P, ["eps"], eps[:], EPS)
        yc = Ring(sb, "yc", [128, 4, 512], F32, 2)
        x0 = Ring(sb, "x0", [128, 4, 512], BF16, 2)
        ya = Ring(sb, "ya", [128, 4, 512], F32, 2)
        yh = Ring(sb, "yh", [128, 4, 512], F32, 2)
        sq = Ring(sb, "sq", [128, 4, 512], BF16, 2)
        pss = Ring(ps, "pss", [128, 512], F32, 2)
        rr = Ring(sb, "rr", [128, 512], F32, 2)
        mx = Ring(sb, "mx", [128, 8, 512], BF16, 2)
        po = Ring(ps, "po", [128, 1024], F32, 2)
        xr = Ring(sb, "xt", [128, 1024], F32, 3)
        x2 = Ring(sb, "x2", [128, 1024], F32, 3)
        nst = NT // 512

        def prepA(st):
            tok0 = st * 512
            c_, ck = yc.get(st)
            x_, xk = x0.get(st)
            a_, ak = ya.get(st)
            P.dma("sp", c_[:], D["YC"][:, tok0:tok0 + 512].rearrange("(c p) t -> p c t", p=128), writes=[ck])
            P.dma("sp", x_[:], D["X0"][:, tok0:tok0 + 512].rearrange("(c p) t -> p c t", p=128), writes=[xk])
            P.dma("sp", a_[:], D["YATT"][:, tok0:tok0 + 512].rearrange("(c p) t -> p c t", p=128), writes=[ak])
            h_, hk = yh.get(st)
            TT(P, [ck, xk], [hk], h_[:], c_[:], x_[:], ALU.mult)
            for half, (src, srck) in enumerate(((h_, hk), (a_, ak))):
                s_, sk = sq.get(2 * st + half)
                ACT(P, [srck], [sk], s_[:], src[:], AF.Square)

        def prepB(st):
            a_, ak = ya.get(st)
            h_, hk = yh.get(st)
            m_, mk = mx.get(st)
            for half, (src, srck) in enumerate(((h_, hk), (a_, ak))):
                s_, sk = sq.get(2 * st + half)
                p_, pk = pss.get(2 * st + half)
                for c in range(4):
                    MM(P, [sk, "onesm"], [pk], p_[:], onesm[:], s_[:, c, :], start=(c == 0), stop=(c == 3))
                r_, rk = rr.get(2 * st + half)
                ACT(P, [pk, "eps"], [rk], r_[:], p_[:], AF.Ln, bias=eps[:], scale=1.0)
                ACT(P, [rk], [rk], r_[:], r_[:], AF.Exp, scale=-0.5)
                for c in range(4):
                    TT(P, [srck, rk], ["%s_%d" % (mk, half * 4 + c)], m_[:, half * 4 + c, :], src[:, c, :], r_[:],
                       ALU.mult, eng=("dve" if c % 2 == 0 else "pool"))

        def mm_tile(st, j):
            tok0 = st * 512
            m_, mk = mx.get(st)
            mkeys = ["%s_%d" % (mk, c) for c in range(8)]
            ti = st * 4 + j
            xt, xtk = xr.get(ti)
            P.dma("sp", xt[:], D["xs"][tok0 + j * 128:tok0 + (j + 1) * 128, :], writes=[xtk])
            o_, ok = po.get(ti)
            for hf in range(2):
                for c in range(8):
                    MM(P, mkeys + WoK(hf * 512, (hf + 1) * 512), [ok], o_[:, hf * 512:(hf + 1) * 512], m_[:, c, j * 128:(j + 1) * 128],
                       Wo[:, c, hf * 512:(hf + 1) * 512], start=(c == 0), stop=(c == 7))
            y_, yk = x2.get(ti)
            TT(P, [ok, xtk], [yk], y_[:], o_[:], xt[:], ALU.add)
            P.dma("pool", D["X2"][tok0 + j * 128:tok0 + (j + 1) * 128, :], y_[:], reads=[yk])

        prepA(0)
        prepB(0)
        for st in range(nst):
            if st + 1 < nst:
                prepA(st + 1)
            mm_tile(st, 0)
            mm_tile(st, 1)
            if st + 1 < nst:
                prepB(st + 1)
            mm_tile(st, 2)
            mm_tile(st, 3)
        P.flush()


def phase_E1(P, nc, D):
    with ExitStack() as ph:
        sb, ps = allocators(nc, ph)
        wst = Ring(sb, "wst", [128, 704], F32, 4)
        gf = sb("gf", [128, 8], F32)
        P.dma("sp", gf[:], D["g_ffn"], writes=["gf"])
        ident = sb("ident", [128, 128], BF16)
        eps = sb("eps", [128, 1], F32)
        P.dma("sp", ident[:], D["ident"], writes=["ident"])
        MS(P, ["eps"], eps[:], EPS)
        xr = Ring(sb, "xt", [128, 1024], F32, 3)
        junk = sb("junk", [128, 1024], BF16)
        ssq = Ring(sb, "ssq", [128, 1], F32, 4)
        xnr = Ring(sb, "xn", [128, 1024], BF16, 2)
        pTr = Ring(ps, "pT", [128, 8, 128], BF16, 2)
        hT = Ring(sb, "hT", [128, 8, 512], BF16, 2)
        pg = Ring(ps, "pg", [128, 512], F32, 3)
        pu = Ring(ps, "pu", [128, 512], F32, 3)
        sg = Ring(sb, "sg", [128, 512], F32, 3)
        ao = Ring(sb, "ao", [128, 512], BF16, 4)
        rings = (ssq, xnr, pTr, junk)
        nst = NT // 512

        def prep1(st, j):
            tok0 = st * 512
            ti = st * 4 + j
            xt, xk = xr.get(ti)
            P.dma("sp", xt[:], D["X2"][tok0 + j * 128:tok0 + (j + 1) * 128, :], writes=[xk])
            norm_part1(P, xt, xk, ti, rings, eps)

        def prep2(st, j):
            hTt, hTk = hT.get(st)
            norm_part2(P, st * 4 + j, j, rings, ident, hTt, hTk)

        for j in range(4):
            prep1(0, j)
            prep2(0, j)
        Wg = sb("Wg", [128, 8, 2816], BF16)
        Wu = sb("Wu", [128, 8, 2816], BF16)
        WgK, WuK = WeightKeys("Wg", 704, 4), WeightKeys("Wu", 704, 4)
        wn = 0
        for cb in range(4):
            for (W_, nm, dr) in ((Wg, "Wg", D["w_gate"]), (Wu, "Wu", D["w_up"])):
                for k in range(8):
                    st_, stk = wst.get(wn)
                    wn += 1
                    P.dma("sp", st_[:], dr[k * 128:(k + 1) * 128, cb * 704:(cb + 1) * 704], writes=[stk])
                    if wn % 2 == 0:
                        ACT(P, [stk, "gf"], ["%s_c%da" % (nm, cb)], W_[:, k, cb * 704:(cb + 1) * 704], st_[:], AF.Copy,
                            scale=gf[:, k:k + 1])
                    else:
                        TS(P, [stk, "gf"], ["%s_c%d" % (nm, cb)], W_[:, k, cb * 704:(cb + 1) * 704], st_[:], gf[:, k:k + 1],
                           None, ALU.mult)
        m = 0
        for st in range(nst):
            tok0 = st * 512
            hTt, hTk = hT.get(st)
            hkeys = ["%s_%d" % (hTk, j) for j in range(4)]
            for fc in range(22):
                g_, gk = pg.get(m)
                u_, uk = pu.get(m)
                for k in range(8):
                    MM(P, hkeys + WgK(fc * 128, (fc + 1) * 128), [gk], g_[:], Wg[:, k, fc * 128:(fc + 1) * 128], hTt[:, k, :],
                       start=(k == 0), stop=(k == 7))
                for k in range(8):
                    MM(P, hkeys + WuK(fc * 128, (fc + 1) * 128), [uk], u_[:], Wu[:, k, fc * 128:(fc + 1) * 128], hTt[:, k, :],
                       start=(k == 0), stop=(k == 7))
                s_, sk = sg.get(m)
                ACT(P, [gk], [sk], s_[:], g_[:], AF.Silu)
                a_, ak = ao.get(m)
                TT(P, [sk, uk], [ak], a_[:], s_[:], u_[:], ALU.mult)
                P.dma("pool", D["ACTD"][fc * 128:(fc + 1) * 128, tok0:tok0 + 512], a_[:], reads=[ak])
                m += 1
                if st + 1 < nst:
                    if fc in (1, 6, 11, 16):
                        prep1(st + 1, (fc - 1) // 5)
                    if fc in (4, 9, 14, 19):
                        prep2(st + 1, (fc - 4) // 5)
        P.flush()


def phase_E2(P, nc, D):
    with ExitStack() as ph:
        sb, ps = allocators(nc, ph)
        wst = Ring(sb, "wst", [128, 512], F32, 4)
        Wd, WdK = load_scaled_weight(P, sb, "Wd", D["w_down"], 22, 1024, None, None, wst, 512)
        gfin = sb("gfin", [128, 1024], F32)
        eps = sb("eps", [128, 1], F32)
        P.dma("sp", gfin[:], D["g_fin"], writes=["gfin"])
        MS(P, ["eps"], eps[:], EPS)
        ar = Ring(sb, "ar", [128, 22, 512], BF16, 2)
        xr = Ring(sb, "xt", [128, 1024], F32, 3)
        po = Ring(ps, "po", [128, 1024], F32, 2)
        yy = Ring(sb, "yy", [128, 1024], F32, 3)
        junk = sb("junk", [128, 1024], BF16)
        ssq = Ring(sb, "ssq", [128, 1], F32, 4)
        for st in range(NT // 512):
            tok0 = st * 512
            a_, ak = ar.get(st)
            P.dma("sp", a_[:], D["ACTD"][:, tok0:tok0 + 512].rearrange("(c p) t -> p c t", p=128), writes=[ak])
            for j in range(4):
                ti = st * 4 + j
                xt, xk = xr.get(ti)
                P.dma("sp", xt[:], D["X2"][tok0 + j * 128:tok0 + (j + 1) * 128, :], writes=[xk])
                o_, ok = po.get(ti)
                for hf in range(2):
                    for c in range(22):
                        MM(P, [ak] + WdK(hf * 512, (hf + 1) * 512), [ok], o_[:, hf * 512:(hf + 1) * 512], a_[:, c, j * 128:(j + 1) * 128],
                           Wd[:, c, hf * 512:(hf + 1) * 512], start=(c == 0), stop=(c == 21))
                y_, yk = yy.get(ti)
                TT(P, [ok, xk], [yk], y_[:], o_[:], xt[:], ALU.add)
                sq, sqk = ssq.get(ti)
                ACT(P, [yk], [sqk], junk[:], y_[:], AF.Square, accum_out=sq[:])
                ACT(P, [sqk, "eps"], [sqk], sq[:], sq[:], AF.Ln, bias=eps[:], scale=1.0 / 1024)
                ACT(P, [sqk], [sqk], sq[:], sq[:], AF.Exp, scale=-0.5)
                STT(P, [yk, sqk, "gfin"], [yk], y_[:], y_[:], sq[:], gfin[:], ALU.mult, ALU.mult)
                P.dma("pool", D["out"][tok0 + j * 128:tok0 + (j + 1) * 128, :], y_[:], reads=[yk])
        P.flush()


INPUT_SPECS = None


def input_specs():
    sp = {
        "xs": ([NT, 1024], F32), "w_in": ([1024, 2432], F32), "g_mix": ([128, 8], F32),
        "cw": ([128, 4, 12], F32), "fw1": ([33, 64], F32), "fb1": ([64, 1], F32), "fw2": ([64, 64], F32),
        "fb2": ([64, 1], F32), "fw3": ([64, 1024], F32), "ffr": ([64, 1], F32), "fdec": ([128, 8], F32),
        "fd": ([128, 4], F32), "gqk": ([128, 2], F32), "g_out": ([128, 8], F32), "w_out": ([1024, 1024], F32),
        "g_ffn": ([128, 8], F32), "w_gate": ([1024, 2816], F32), "w_up": ([1024, 2816], F32),
        "w_down": ([2816, 1024], F32), "g_fin": ([128, 1024], F32),
        "ident": ([128, 128], BF16), "pmat": ([128, 128], F32), "bones": ([128, 128], BF16),
        "ones512": ([128, 128], BF16), "sel": ([65, 64], F32),
        "ropec": ([128, 8192], F32), "ropes": ([128, 8192], F32),
        "c2": ([128, 128], BF16), "s2": ([128, 128], BF16), "s2n": ([128, 128], BF16),
        "cs2": ([128, 256], BF16), "sc2": ([128, 256], BF16),
    }
    for li, (L, N1, nblk, G) in enumerate(LENS):
        s = "_%d" % li
        sp["zt%d" % li] = ([33, L], F32)
        sp["trow%d" % li] = ([128, L], F32)
        sp["f1cat" + s] = ([nblk, 2 * N1], BF16)
        sp["tw" + s] = ([128, 2, G * N1], F32)
        sp["twT" + s] = ([128, 2, 512], F32)
        sp["c1i" + s] = ([128, 64], BF16)
        sp["s1ni" + s] = ([128, 64], BF16)
    return sp


def scratch_specs():
    sp = {
        "U": ([1536, NT], BF16), "QT": ([512, NT], BF16), "KT": ([256, NT], BF16), "VA": ([NT, 130], BF16),
        "Z": ([512, NT], BF16), "X0": ([512, NT], BF16), "YC": ([512, NT], F32), "YATT": ([512, NT], F32),
        "X2": ([NT, 1024], F32), "ACTD": ([2816, NT], BF16),
    }
    for li, (L, N1, nblk, G) in enumerate(LENS):
        sp["KRAW%d" % li] = ([1024, L], F32)
        sp["KF%d" % li] = ([1024, L], BF16)
        sp["RN%d" % li] = ([128, 4], F32)
        sp["KSPEC%d" % li] = ([128, 512, 2, N1], F32)
    return sp


def build(debug=(), phases=None):
    nc = bass.Bass("TRN2", target_bir_lowering=False)
    D = {}
    for name, (shape, dt) in input_specs().items():
        D[name] = nc.dram_tensor(name, list(shape), dt, kind="ExternalInput").ap()
    for name, (shape, dt) in scratch_specs().items():
        kind = "ExternalOutput" if name in debug else "Internal"
        D[name] = nc.dram_tensor(name, list(shape), dt, kind=kind).ap()
    D["out"] = nc.dram_tensor("out", [NT, 1024], F32, kind="ExternalOutput").ap()
    allp = ["A", "F1", "F2", "F3", "B1", "B2", "C", "D", "E1", "E2"]
    phases = allp if phases is None else phases
    with ExitStack() as es:
        P = Prog(nc, es)
        if "A" in phases:
            phase_A(P, nc, D)
        for li in range(2):
            if "F1" in phases:
                phase_F1(P, nc, D, li)
        fused = ("C" in phases and "B1" in phases and "F2" in phases and FUSE_DVE_WORK)
        if fused:
            def extra(sb):
                u1 = b1_units(P, nc, D, sb)
                u2 = f2_units(P, nc, D, 0, sb) + f2_units(P, nc, D, 1, sb)
                out = []
                while u1 or u2:
                    if u2:
                        out.append(u2.pop(0))
                    if u1 and (len(u1) * 5 >= len(u2) * 3 or not u2):
                        out.append(u1.pop(0))
                return out
            phase_C(P, nc, D, [0, 1, 2], extra=extra)
        else:
            for li in range(2):
                if "F2" in phases:
                    phase_F2(P, nc, D, li)
            if "B1" in phases:
                phase_B1(P, nc, D)
        for li in range(2):
            if "F3" in phases:
                phase_F3(P, nc, D, li)
        for si in range(3):
            if "B2" in phases:
                phase_B2(P, nc, D, si)
        if "C" in phases and not fused:
            phase_C(P, nc, D, [0, 1, 2])
        if "D" in phases:
            phase_D(P, nc, D)
        if "E1" in phases:
            phase_E1(P, nc, D)
        if "E2" in phases:
            phase_E2(P, nc, D)
    return nc


def _bf(a):
    return np.ascontiguousarray(a).astype(ml_dtypes.bfloat16)


def host_consts():
    c = {}
    c["ident"] = _bf(np.eye(128))
    pm = np.zeros((128, 128), np.float32)
    for i in range(64):
        pm[2 * i + 1, 2 * i] = -1.0
        pm[2 * i, 2 * i + 1] = 1.0
    c["pmat"] = pm
    bo = np.zeros((128, 128), np.float32)
    bo[:64, :64] = 1.0 / 64
    bo[64:, 64:] = 1.0 / 64
    c["bones"] = _bf(bo)
    c["ones512"] = _bf(np.full((128, 128), 1.0 / 512))
    sel = np.zeros((65, 64), np.float32)
    sel[64, :] = 1.0
    c["sel"] = sel
    t = np.arange(8192)
    row = (t // 64).astype(np.float32)
    col = (t % 64).astype(np.float32)
    inv = (10000.0 ** (-np.arange(0, 32, 2, dtype=np.float32) / 32)).astype(np.float32)
    ang = np.concatenate([row[:, None] * inv, col[:, None] * inv], axis=-1).astype(np.float32)
    cosT = np.cos(ang.astype(np.float64)).T
    sinT = np.sin(ang.astype(np.float64)).T
    idx = (np.arange(128) % 64) // 2
    c["ropec"] = np.ascontiguousarray(cosT[idx]).astype(np.float32)
    c["ropes"] = np.ascontiguousarray(sinT[idx]).astype(np.float32)
    s2 = np.arange(128)
    a2 = 2 * np.pi * np.outer(s2, s2) / 128
    C2, S2 = np.cos(a2), np.sin(a2)
    c["c2"], c["s2"], c["s2n"] = _bf(C2), _bf(S2), _bf(-S2)
    c["cs2"] = _bf(np.concatenate([C2, S2], 1))
    c["sc2"] = _bf(np.concatenate([-S2, C2], 1))
    for li, (L, N1, nblk, G) in enumerate(LENS):
        s = "_%d" % li
        N = 128 * N1
        tt = np.linspace(0.0, 1.0, L, dtype=np.float32)
        w = (np.float32(2.0 * np.pi / L) * np.arange(L, dtype=np.float32)).astype(np.float32)
        bands = np.linspace(1e-4, 15, 16, dtype=np.float32)
        angf = (w[:, None] * bands[None, :]).astype(np.float32)
        z = np.concatenate([tt[:, None], np.cos(angf.astype(np.float64)), -np.sin(angf.astype(np.float64))], -1)
        c["zt%d" % li] = np.ascontiguousarray(z.T).astype(np.float32)
        c["trow%d" % li] = np.ascontiguousarray(np.broadcast_to(tt[None, :], (128, L))).astype(np.float32)
        s1 = np.arange(nblk)
        f1 = np.arange(N1)
        a1 = 2 * np.pi * np.outer(s1, f1) / N1
        c["f1cat" + s] = _bf(np.concatenate([np.cos(a1), -np.sin(a1)], 1))
        aw = 2 * np.pi * np.outer(s2, f1) / N
        tw = np.stack([np.tile(np.cos(aw), (1, G)), np.tile(-np.sin(aw), (1, G))], 1)
        c["tw" + s] = tw.astype(np.float32)
        awT = aw.T
        rep = 128 // N1
        twT = np.stack([np.tile(np.cos(awT), (rep, 4)), np.tile(np.sin(awT), (rep, 4))], 1)
        c["twT" + s] = twT.astype(np.float32)
        c1 = np.zeros((128, 64))
        s1 = np.zeros((128, 64))
        for r_ in range(rep):
            c1[r_ * N1:(r_ + 1) * N1, r_ * nblk:(r_ + 1) * nblk] = np.cos(a1.T) / N
            s1[r_ * N1:(r_ + 1) * N1, r_ * nblk:(r_ + 1) * nblk] = -np.sin(a1.T) / N
        c["c1i" + s] = _bf(c1)
        c["s1ni" + s] = _bf(s1)
    return c


def host_weights(inp):
    f = lambda k: np.asarray(inp[k], np.float32)
    w = {}
    win = f("w_in")[0]
    kc = win[:, 2048:2176]
    w["w_in"] = np.ascontiguousarray(np.concatenate(
        [win[:, :2048], kc[:, :64], kc[:, :64], kc[:, 64:], kc[:, 64:], win[:, 2176:2304]], 1))
    pk = lambda v: np.ascontiguousarray(v.reshape(-1, 128).T)
    w["g_mix"] = pk(f("norm_mix_g")[0])
    cwb = np.concatenate([f("hy_conv_w")[0], f("hy_conv_b")], 0)
    w["cw"] = np.ascontiguousarray(cwb.reshape(4, 12, 128).transpose(2, 0, 1))
    w["fw1"] = f("hy_f_w1")[0]
    w["fb1"] = f("hy_f_b1")[0].reshape(64, 1)
    w["fw2"] = f("hy_f_w2")[0]
    w["fb2"] = f("hy_f_b2")[0].reshape(64, 1)
    w["fw3"] = f("hy_f_w3")[0]
    w["ffr"] = f("hy_f_freq")[0].reshape(64, 1)
    w["fdec"] = np.ascontiguousarray(f("hy_decay")[0].reshape(2, 4, 128).transpose(2, 0, 1).reshape(128, 8))
    w["fd"] = pk(f("hy_d")[0])
    w["gqk"] = np.ascontiguousarray(np.stack([np.tile(f("q_norm_g")[0], 2), np.tile(f("k_norm_g")[0], 2)], 1))
    w["g_out"] = pk(np.concatenate([f("hy_out_g")[0], f("att_out_g")[0]]))
    w["w_out"] = f("w_out")[0]
    w["g_ffn"] = pk(f("norm_ffn_g")[0])
    w["w_gate"] = f("w_gate")[0]
    w["w_up"] = f("w_up")[0]
    w["w_down"] = f("w_down")[0]
    w["g_fin"] = np.ascontiguousarray(np.broadcast_to(f("final_norm_g")[None, :], (128, 1024)))
    return {k: np.ascontiguousarray(v, dtype=np.float32) for k, v in w.items()}


def make_in_maps(inp, cores=range(8)):
    shared = dict(host_consts())
    shared.update(host_weights(inp))
    xp = np.asarray(inp["x_prompt"], np.float32)
    xsm = np.asarray(inp["x_sample"], np.float32)
    maps = []
    for i in cores:
        m = dict(shared)
        m["xs"] = np.ascontiguousarray(np.concatenate([xsm[i], xp[2 * i], xp[2 * i + 1]], 0))
        maps.append(m)
    return maps


_NC_CACHE = {}


def kernel(**inputs):
    if "nc" not in _NC_CACHE:
        _NC_CACHE["nc"] = build()
    nc = _NC_CACHE["nc"]
    maps = make_in_maps(inputs)
    res = run_bass_kernel_spmd(nc, maps, core_ids=list(range(8)))
    y_prompt = np.empty((16, 2048, 1024), np.float32)
    y_sample = np.empty((8, 8192, 1024), np.float32)
    for i in range(8):
        o = np.asarray(res.results[i]["out"], np.float32)
        y_sample[i] = o[:8192]
        y_prompt[2 * i] = o[8192:10240]
        y_prompt[2 * i + 1] = o[10240:12288]
    return (y_prompt, y_sample)
```
